# Optimizing a Trainium2 kernel written in Bass

```python
import math
import jax, jax.numpy as jnp
from jax import lax
import numpy as np

D_MODEL = 1024
BATCH = 2
SEQ = 8192
DEPTH = 4
DEC_BATCH = 1
DEC_SEQ = 16384
PAST_LEN = 128

N_EVEN = (DEPTH + 1) // 2
N_ODD = DEPTH // 2
POOL_WIDTH = D_MODEL // 2
N_POOL_GROUPS = 4
POOL_GROUP_DIM = POOL_WIDTH // N_POOL_GROUPS
POOL_WINDOWS = (2, 4, 8, 16)
CONV_WIDTH_CH = D_MODEL // 2
CONV_K = 3
MIX_IN_COLS = POOL_WIDTH + 3 * CONV_WIDTH_CH
MIX_OUT_IN = POOL_WIDTH + CONV_WIDTH_CH
N_HEADS = 8
HEAD_DIM = D_MODEL // (2 * N_HEADS)
D_ATTN = 2 * N_HEADS * HEAD_DIM
ATTN_SCALE = HEAD_DIM ** -0.5
Q_BLOCK = 128
D_FF = 4 * D_MODEL
NORM_EPS = 1e-6

kernel_name = "hybrid_pool_shortconv_diffattn_encoder"


def rms_norm(x, g, eps=NORM_EPS):
    xf = x.astype(jnp.float32)
    y = xf * lax.rsqrt(jnp.mean(xf * xf, axis=-1, keepdims=True) + eps)
    return (y * g.astype(jnp.float32)).astype(x.dtype)


def centred_mean(u, w):
    S = u.shape[1]
    cs = jnp.cumsum(u.astype(jnp.float32), axis=1)
    cs = jnp.pad(cs, ((0, 0), (1, 0), (0, 0)))
    t = jnp.arange(S)
    lo = jnp.clip(t - w // 2, 0, S)
    hi = jnp.clip(t + w // 2, 0, S)
    total = cs[:, hi] - cs[:, lo]
    cnt = (hi - lo).astype(jnp.float32)
    return (total / cnt[None, :, None]).astype(u.dtype)


def pool_mixer(u, pool_w, pool_scale):
    B_, S, _ = u.shape
    ug = u.reshape(B_, S, N_POOL_GROUPS, POOL_GROUP_DIM)
    diffs = [centred_mean(ug[:, :, g], w) - ug[:, :, g] for g, w in enumerate(POOL_WINDOWS)]
    d = jnp.stack(diffs, axis=2)
    y = jnp.einsum('bsgc,gcd->bsgd', d, pool_w)
    return y.reshape(B_, S, POOL_WIDTH) * pool_scale


def short_conv_mixer(h, gate_b, gate_c, conv_w):
    z = gate_c * h
    zp = jnp.pad(z, ((0, 0), (1, 1), (0, 0)))
    conv = conv_w[0] * zp[:, :-2] + conv_w[1] * zp[:, 1:-1] + conv_w[2] * zp[:, 2:]
    return gate_b * conv


def even_mixer(h, mix_in_w, pool_w, pool_scale, conv_w, mix_out_w):
    proj = h @ mix_in_w
    u_pool = proj[..., :POOL_WIDTH]
    c0 = POOL_WIDTH
    h_conv = proj[..., c0:c0 + CONV_WIDTH_CH]
    g_b = proj[..., c0 + CONV_WIDTH_CH:c0 + 2 * CONV_WIDTH_CH]
    g_c = proj[..., c0 + 2 * CONV_WIDTH_CH:]
    a_out = pool_mixer(u_pool, pool_w, pool_scale)
    b_out = short_conv_mixer(h_conv, g_b, g_c, conv_w)
    return jnp.concatenate([a_out, b_out], axis=-1) @ mix_out_w


def diff_attention(h, qkv_w, out_w, lq1, lk1, lq2, lk2, subln_g, lambda_init):
    B_, S, _ = h.shape
    qkv = h @ qkv_w
    q, k, v = jnp.split(qkv, 3, axis=-1)
    q = q.reshape(B_, S, 2 * N_HEADS, HEAD_DIM)
    k = k.reshape(B_, S, 2 * N_HEADS, HEAD_DIM)
    v = v.reshape(B_, S, N_HEADS, 2 * HEAD_DIM)
    f32 = jnp.float32
    lam = (jnp.exp(jnp.sum(lq1.astype(f32) * lk1.astype(f32)))
           - jnp.exp(jnp.sum(lq2.astype(f32) * lk2.astype(f32))) + lambda_init)
    head_slopes = jnp.exp2(-8.0 * (jnp.arange(N_HEADS, dtype=f32) + 1.0) / N_HEADS)
    map_slopes = jnp.repeat(head_slopes, 2)
    nb = S // Q_BLOCK
    qb = q.reshape(B_, nb, Q_BLOCK, 2 * N_HEADS, HEAD_DIM).transpose(1, 0, 2, 3, 4)
    starts = jnp.arange(nb) * Q_BLOCK
    key_pos = jnp.arange(S)

    def block(args):
        qblk, t0 = args
        s = jnp.einsum('bqmd,bkmd->bmqk', qblk, k, preferred_element_type=f32) * ATTN_SCALE
        dist = jnp.abs((t0 + jnp.arange(Q_BLOCK))[:, None] - key_pos[None, :]).astype(f32)
        p = jax.nn.softmax(s - map_slopes[:, None, None] * dist, axis=-1)
        p = p.reshape(B_, N_HEADS, 2, Q_BLOCK, S)
        a = p[:, :, 0] - lam * p[:, :, 1]
        return jnp.einsum('bhqk,bkhe->bqhe', a.astype(v.dtype), v)

    o = lax.map(block, (qb, starts))
    o = o.transpose(1, 0, 2, 3, 4).reshape(B_, S, N_HEADS, 2 * HEAD_DIM)
    o = rms_norm(o, subln_g, eps=1e-5) * (1.0 - lambda_init)
    return o.reshape(B_, S, D_ATTN) @ out_w


def sq_relu_mlp(h, w1, w2):
    a = jax.nn.relu(h @ w1)
    return (a * a) @ w2


def setup_inputs(seed: int = 0) -> dict:
    key = jax.random.key(seed)
    ks = jax.random.split(key, 20)
    n = jax.random.normal
    f32 = jnp.float32
    return {
        "x_prompt": n(ks[0], (BATCH, SEQ, D_MODEL), f32),
        "x_sample": n(ks[1], (DEC_BATCH, DEC_SEQ, D_MODEL), f32),
        "norm1_g": 1.0 + 0.02 * n(ks[2], (DEPTH, D_MODEL), f32),
        "norm2_g": 1.0 + 0.02 * n(ks[3], (DEPTH, D_MODEL), f32),
        "final_g": 1.0 + 0.02 * n(ks[4], (D_MODEL,), f32),
        "mix_in_w": n(ks[5], (N_EVEN, D_MODEL, MIX_IN_COLS), f32) * D_MODEL ** -0.5,
        "pool_w": n(ks[6], (N_EVEN, N_POOL_GROUPS, POOL_GROUP_DIM, POOL_GROUP_DIM), f32) * POOL_GROUP_DIM ** -0.5,
        "pool_scale": 1.0 + 0.1 * n(ks[7], (N_EVEN, POOL_WIDTH), f32),
        "conv_w": n(ks[8], (N_EVEN, CONV_K, CONV_WIDTH_CH), f32) * CONV_K ** -0.5,
        "mix_out_w": n(ks[9], (N_EVEN, MIX_OUT_IN, D_MODEL), f32) * MIX_OUT_IN ** -0.5,
        "attn_qkv_w": n(ks[10], (N_ODD, D_MODEL, 3 * D_ATTN), f32) * D_MODEL ** -0.5,
        "attn_out_w": n(ks[11], (N_ODD, D_ATTN, D_MODEL), f32) * D_ATTN ** -0.5,
        "lambda_q1": 0.1 * n(ks[12], (N_ODD, HEAD_DIM), f32),
        "lambda_k1": 0.1 * n(ks[13], (N_ODD, HEAD_DIM), f32),
        "lambda_q2": 0.1 * n(ks[14], (N_ODD, HEAD_DIM), f32),
        "lambda_k2": 0.1 * n(ks[15], (N_ODD, HEAD_DIM), f32),
        "subln_g": 1.0 + 0.02 * n(ks[16], (N_ODD, 2 * HEAD_DIM), f32),
        "mlp_w1": n(ks[17], (DEPTH, D_MODEL, D_FF), f32) * D_MODEL ** -0.5,
        "mlp_w2": n(ks[18], (DEPTH, D_FF, D_MODEL), f32) * D_FF ** -0.5,
    }


def trunk(x, norm1_g, norm2_g, final_g, mix_in_w, pool_w, pool_scale, conv_w, mix_out_w,
          attn_qkv_w, attn_out_w, lambda_q1, lambda_k1, lambda_q2, lambda_k2, subln_g,
          mlp_w1, mlp_w2):
    for i in range(DEPTH):
        h = rms_norm(x, norm1_g[i])
        j = i // 2
        if i % 2 == 0:
            x = x + even_mixer(h, mix_in_w[j], pool_w[j], pool_scale[j], conv_w[j], mix_out_w[j])
        else:
            lambda_init = 0.8 - 0.6 * math.exp(-0.3 * i)
            x = x + diff_attention(h, attn_qkv_w[j], attn_out_w[j], lambda_q1[j], lambda_k1[j],
                                   lambda_q2[j], lambda_k2[j], subln_g[j], lambda_init)
        x = x + sq_relu_mlp(rms_norm(x, norm2_g[i]), mlp_w1[i], mlp_w2[i])
    return rms_norm(x, final_g)


def reference(x_prompt, x_sample, norm1_g, norm2_g, final_g, mix_in_w, pool_w, pool_scale,
              conv_w, mix_out_w, attn_qkv_w, attn_out_w, lambda_q1, lambda_k1, lambda_q2,
              lambda_k2, subln_g, mlp_w1, mlp_w2):
    y_prompt = trunk(x_prompt, norm1_g, norm2_g, final_g, mix_in_w, pool_w, pool_scale, conv_w,
                     mix_out_w, attn_qkv_w, attn_out_w, lambda_q1, lambda_k1, lambda_q2,
                     lambda_k2, subln_g, mlp_w1, mlp_w2)
    y_sample = trunk(x_sample, norm1_g, norm2_g, final_g, mix_in_w, pool_w, pool_scale, conv_w,
                     mix_out_w, attn_qkv_w, attn_out_w, lambda_q1, lambda_k1, lambda_q2,
                     lambda_k2, subln_g, mlp_w1, mlp_w2)
    return (y_prompt, y_sample)
```

```python
import math
from contextlib import ExitStack
import numpy as np
import ml_dtypes
import concourse.bass as bass
import concourse.mybir as mybir
from concourse.bass_utils import run_bass_kernel_spmd

F32, BF16 = mybir.dt.float32, mybir.dt.bfloat16
AF = mybir.ActivationFunctionType
ALU = mybir.AluOpType
BF = ml_dtypes.bfloat16

D = 1024
KC = 8
T = 512
HALO = 8
TP = T + 2 * HALO
DFF = 4096
WINDOWS = (2, 4, 8, 16)
NH = 8
SCALE = 0.125
KB = 1024
UNIT = 4096


class Sem:
    def __init__(self, nc, name):
        self.h = nc.alloc_semaphore(name)
        self.v = 0


class Buf:
    def __init__(self):
        self.w = {}
        self.r = {}


class Q:
    def __init__(self, nc, eng, name):
        self.e = eng
        self.sem = Sem(nc, name)
        self.seen = {}

    def wait_tok(self, sem, v):
        if self.seen.get(sem, 0) >= v:
            return
        self.e.wait_ge(sem.h, v)
        self.seen[sem] = v

    def deps(self, reads, writes):
        for b in reads:
            for sem, v in b.w.items():
                self.wait_tok(sem, v)
        for b in writes:
            for sem, v in b.w.items():
                self.wait_tok(sem, v)
            for sem, v in b.r.items():
                self.wait_tok(sem, v)

    def finish(self, inst, reads, writes, sem=None, inc=1):
        sem = sem or self.sem
        sem.v += inc
        inst.then_inc(sem.h, inc)
        for b in reads:
            b.r[sem] = max(b.r.get(sem, 0), sem.v)
        for b in writes:
            b.w = {sem: sem.v}
            b.r = {}
        return (sem, sem.v)

    def op(self, fn, reads=(), writes=()):
        self.deps(reads, writes)
        return self.finish(fn(self.e), reads, writes)

    def dma(self, out, in_, reads, writes, sem):
        self.deps(reads, writes)
        return self.finish(self.e.dma_start(out=out, in_=in_), reads, writes, sem=sem, inc=16)


class Ctx:
    def __init__(self, nc):
        self.nc = nc
        self.pe = Q(nc, nc.tensor, "q_pe")
        self.act = Q(nc, nc.scalar, "q_act")
        self.dve = Q(nc, nc.vector, "q_dve")
        self.pool = Q(nc, nc.gpsimd, "q_pool")
        self.sp = Q(nc, nc.sync, "q_sp")
        self._n = 0
        self.out_sems = []
        self.all_sems = []

    def name(self, p):
        self._n += 1
        return f"{p}{self._n}"

    def sb(self, shape, dt, p="t", es=None):
        if es is not None:
            return es.enter_context(self.nc.sbuf_tensor(self.name(p), list(shape), dt))
        return self.nc.alloc_sbuf_tensor(self.name(p), list(shape), dt)

    def ps(self, p="ps"):
        return self.nc.alloc_psum_tensor(self.name(p), [128, 512], F32)

    def sem(self, p="s"):
        s = Sem(self.nc, self.name(p))
        self.all_sems.append(s)
        return s

    def barrier(self):
        qs = [self.pe, self.act, self.dve, self.pool, self.sp]
        for q in qs:
            for o in qs:
                if o is not q and o.sem.v > 0:
                    q.wait_tok(o.sem, o.sem.v)
            for s in self.all_sems:
                if s.v > 0:
                    q.wait_tok(s, s.v)

    def mm_group(self, psb, mms, reads):
        self.pe.deps(reads, [psb])
        n = len(mms)
        inst = None
        for i, (o, l, r) in enumerate(mms):
            inst = self.nc.tensor.matmul(o, lhsT=l, rhs=r, start=(i == 0), stop=(i == n - 1))
        return self.pe.finish(inst, reads, [psb])

    def finish_outputs(self):
        for s in self.out_sems:
            self.sp.wait_tok(s, s.v)


class Ring:
    def __init__(self, cx, n, shape, dt, p, es=None):
        self.cx = cx
        self.t = [cx.sb(shape, dt, p, es) for _ in range(n)]
        self.b = [Buf() for _ in range(n)]
        self.s = [cx.sem(p + "s") for _ in range(n)]
        self.i = 0
        self.n = n

    def load(self, q, src_ap):
        i = self.i
        self.i = (self.i + 1) % self.n
        q.dma(self.t[i][:], src_ap, [], [self.b[i]], self.s[i])
        return self.t[i], self.b[i]


class Stream:
    def __init__(self, cx, q, rings, srcs):
        self.q = q
        self.rings = rings
        self.srcs = srcs
        self.ni = 0
        self.ng = 0
        self.slots = {}

    def _issue(self):
        i = self.ni
        self.slots[i] = [r.load(self.q, a) for r, a in zip(self.rings, self.srcs[i])]
        self.ni += 1

    def get(self):
        i = self.ng
        la = self.rings[0].n - 2
        while self.ni < len(self.srcs) and self.ni <= i + la:
            self._issue()
        self.ng += 1
        return self.slots.pop(i)


def rms_stats(cx, xt, xb, lo, n, sq, sqb, ps_list, eps_t, rstd, rstdb, ones, onesb, tmpb=None):
    cx.act.op(lambda e: e.activation(out=sq[:, :, 0:n], in_=xt[:, :, lo:lo + n], func=AF.Square),
              [xb], [sqb])
    c0 = 0
    k = 0
    while c0 < n:
        cn = min(512, n - c0)
        pt, pb = ps_list[k]
        cx.mm_group(pb, [(pt[:, 0:cn], ones[:], sq[:, kc, c0:c0 + cn]) for kc in range(KC)],
                    [sqb, onesb])
        cx.act.op(lambda e: e.activation(out=rstd[:, c0:c0 + cn], in_=pt[:, 0:cn], func=AF.Ln,
                                         bias=eps_t[:, 0:1], scale=1.0), [pb], [rstdb])
        cx.act.op(lambda e: e.activation(out=rstd[:, c0:c0 + cn], in_=rstd[:, c0:c0 + cn],
                                         func=AF.Exp, scale=-0.5), [rstdb], [rstdb])
        c0 += cn
        k += 1


def mlp_block(cx, xt, xb, xlo, h2, h2b, ws, pA, pB, relu_t, relu_b, aT, aTb):
    kA = 0
    kB = 0
    for fb in range(4):
        wu = []
        for j in range(2):
            wu.append(ws.get()[0])
        for f in range(8):
            wt, wb = wu[f // 4]
            pt, pb = pA[kA % len(pA)]
            kA += 1
            cx.mm_group(pb, [(pt[:], wt[:, kc * 512 + (f % 4) * 128: kc * 512 + (f % 4) * 128 + 128],
                              h2[:, kc, 0:T]) for kc in range(KC)], [wb] + h2b)
            rt, rb = relu_t[f % len(relu_t)], relu_b[f % len(relu_t)]
            cx.act.op(lambda e: e.activation(out=rt[:], in_=pt[:], func=AF.Relu), [pb], [rb])
            cx.pool.op(lambda e: e.tensor_tensor(out=aT[:, f, :], in0=rt[:], in1=rt[:], op=ALU.mult),
                       [rb], [aTb[f]])
        wv = []
        for j in range(2):
            wv.append(ws.get()[0])
        for n in range(8):
            wt, wb = wv[n // 4]
            pt, pb = pB[kB % len(pB)]
            kB += 1
            cx.mm_group(pb, [(pt[:], wt[:, f * 512 + (n % 4) * 128: f * 512 + (n % 4) * 128 + 128],
                              aT[:, f, :]) for f in range(8)], [wb] + aTb)
            cx.dve.op(lambda e: e.tensor_tensor(out=xt[:, n, xlo:xlo + T], in0=xt[:, n, xlo:xlo + T],
                                                in1=pt[:], op=ALU.add), [pb, xb[n]], [xb[n]])


def apply_norm(cx, xt, xb, lo, n, gcol, rstd, rstdb, h, hb):
    for kc in range(KC):
        q = cx.dve
        q.op(lambda e: e.scalar_tensor_tensor(out=h[:, kc, 0:n], in0=xt[:, kc, lo:lo + n],
                                              scalar=gcol(kc), in1=rstd[:, 0:n],
                                              op0=ALU.mult, op1=ALU.mult),
             [xb[kc], rstdb], [hb[kc]])


NPA = 40
NPB = 8 + 8 + 1 + 256 + 2
NU = 92
QUADS = [[0, 1, 2, 3], [4, 5, 6, 7]]
PAIRS = [[0, 4], [1, 5], [2, 6], [3, 7]]
W_IN = {0: 0, 2: 46}
W_OUT = {0: 4, 2: 50}
W_QKV = {1: 22, 3: 68}
W_AO = {1: 28, 3: 74}
W_1 = {0: 6, 1: 30, 2: 52, 3: 76}
W_2 = {0: 14, 1: 38, 2: 60, 3: 84}


def mlp_units(i):
    return [b + 2 * fb + j for fb in range(4) for b in (W_1[i], W_2[i]) for j in (0, 1)]


def build_fused(tile_seq, seq_len):
    nt = len(tile_seq)
    ns = len(seq_len)
    NT = nt * T
    nc = bass.Bass("TRN2", target_bir_lowering=False)

    def ext(name, shape, dt, out=False):
        return nc.dram_tensor(name, list(shape), dt, kind="ExternalOutput" if out else "ExternalInput").ap()

    xin = ext("xin", [nt, 128, KC, TP], F32)
    cnt = ext("cnt", [nt, 128, 4, T], F32)
    wsrc = ext("wsrc", [NU, 128, UNIT], F32)
    pa_in = ext("pa", [2, 128, NPA], F32)
    pw_in = ext("pw", [2, 128, 4, 128], F32)
    pb_in = ext("pb", [2, 128, NPB], F32)
    nchunks = [seq_len[tile_seq[t]] // 128 for t in range(nt)]
    dbase = [sum(nchunks[:t]) for t in range(nt)]
    ndl = sum(nchunks)
    dl_in = ext("dl", [128, ndl], F32)
    d0_in = ext("d0", [128, 1408], F32)
    cng_in = ext("cng", [128, 16], F32)
    hm_in = ext("hm", [128, 16], F32)
    yout = ext("yout", [nt, 128, KC, T], F32, out=True)

    WB = nc.dram_tensor("WB", [NU, 128, UNIT], BF16).ap()
    WBb = [Buf() for _ in range(NU)]
    XL = nc.dram_tensor("XL", [nt, 128, KC, T], F32).ap()
    XLb = [Buf() for _ in range(nt)]
    QL = nc.dram_tensor("QL", [nt, NH, 128, T], BF16).ap()
    QLb = [Buf() for _ in range(nt)]
    att = []
    for a in range(2):
        d = {}
        d["KTL"] = nc.dram_tensor(f"KTL{a}", [NH * 128, NT], BF16)
        d["VL"] = nc.dram_tensor(f"VL{a}", [NH * 128, nt * 4 * 128], BF16)
        d["G1K"] = nc.dram_tensor(f"G1K{a}", [4 * NH * 128, NT], BF16)
        d["G2K"] = nc.dram_tensor(f"G2K{a}", [8 * NH * 128, NT], BF16)
        d["G1V"] = nc.dram_tensor(f"G1V{a}", [4 * NH * 128, nt * 4 * 128], BF16)
        d["G2V"] = nc.dram_tensor(f"G2V{a}", [8 * NH * 128, nt * 4 * 128], BF16)
        for k in list(d.keys()):
            d[k + "b"] = Buf()
        att.append(d)
    EL = nc.dram_tensor("EL", [nt * 2 * 128, KC * HALO], F32)
    GE1 = nc.dram_tensor("GE1", [4 * nt * 2 * 128, KC * HALO], F32)
    GE2 = nc.dram_tensor("GE2", [8 * nt * 2 * 128, KC * HALO], F32)
    ELb, GE1b, GE2b = Buf(), Buf(), Buf()

    cx = Ctx(nc)
    PS = [(cx.ps(), Buf()) for _ in range(8)]
    ones = cx.sb([128, 128], BF16, "ones"); onesb = Buf()
    cx.pool.op(lambda e: e.memset(ones[:], 1.0 / D), [], [onesb])
    one1 = cx.sb([128, 128], BF16, "one1"); one1b = Buf()
    cx.pool.op(lambda e: e.memset(one1[:], 1.0), [], [one1b])
    onee = cx.sb([128, 128], BF16, "onee"); oneeb = Buf()
    cx.pool.op(lambda e: e.memset(onee[:], 1.0 / 128), [], [oneeb])
    eps_t = cx.sb([128, 2], F32, "eps"); epsb = Buf()
    cx.pool.op(lambda e: e.memset(eps_t[:, 0:1], 1e-6), [], [epsb])
    cx.pool.op(lambda e: e.memset(eps_t[:, 1:2], 1e-5), [], [epsb])
    cx.act.deps([epsb], [])
    wring = Ring(cx, 6, [128, UNIT], BF16, "wr")
    wseq = []
    for i in range(4):
        for t in range(nt):
            if i % 2 == 0:
                wseq += [W_IN[i] + u for u in range(4)] + [W_OUT[i] + u for u in range(2)] + mlp_units(i) \
                    + [W_QKV[i + 1] + u for u in range(6)]
            else:
                wseq += [W_AO[i] + u for u in range(2)] + mlp_units(i)

    class WStream(Stream):
        def _issue(self_):
            i = self_.ni
            u = wseq[i]
            ring = self_.rings[0]
            k = ring.i
            ring.i = (ring.i + 1) % ring.n
            self_.q.dma(ring.t[k][:], WB[u], [WBb[u]], [ring.b[k]], ring.s[k])
            self_.slots[i] = [(ring.t[k], ring.b[k])]
            self_.ni += 1
    ws = WStream(cx, cx.sp, [wring], wseq)
    xt = cx.sb([128, KC, TP], F32, "x"); xwb = Buf()
    xsem = cx.sem("xs")
    sq = cx.sb([128, KC, TP], BF16, "sq"); sqb = Buf()
    rstd = cx.sb([128, TP], F32, "rstd"); rstdb = Buf()
    h = cx.sb([128, KC, TP], BF16, "h"); hb = [Buf() for _ in range(KC)]
    relu_t = [cx.sb([128, T], F32, "rl") for _ in range(3)]; relu_b = [Buf() for _ in range(3)]
    aT = cx.sb([128, 8, T], BF16, "aT"); aTb = [Buf() for _ in range(8)]
    osem = [cx.sem("os") for _ in range(6)]
    cx.out_sems += osem
    xb = [xwb] * KC

    with ExitStack() as es:
        ring = Ring(cx, 2, [128, UNIT], F32, "wst", es)
        ob = [cx.sb([128, UNIT], BF16, "wo", es) for _ in range(2)]
        obb = [Buf() for _ in range(2)]
        obs = [cx.sem("wos") for _ in range(2)]
        for u in range(NU):
            t_, b_ = ring.load(cx.sp, wsrc[u])
            k = u % 2
            if u % 2 == 0:
                cx.dve.op(lambda e: e.tensor_copy(out=ob[k][:], in_=t_[:]), [b_], [obb[k]])
            else:
                cx.act.op(lambda e: e.activation(out=ob[k][:], in_=t_[:], func=AF.Copy), [b_], [obb[k]])
            cx.pool.dma(WB[u], ob[k][:], [obb[k]], [WBb[u]], obs[k])
        cx.barrier()

    ccs = cx.sem("cc")
    ccb = Buf()

    def cc(groups, src_ap, dst_ap, reads, writes):
        cx.pool.deps(reads, writes + [ccb])
        ins = nc.gpsimd.collective_compute("AllGather", ALU.bypass, replica_groups=groups,
                                           ins=[src_ap], outs=[dst_ap])
        cx.pool.finish(ins, reads, writes + [ccb], sem=ccs, inc=1)

    def gather(src, srcb, g1, g1b, g2, g2b):
        cc(QUADS, src[:, :], g1[:, :], [srcb], [g1b])
        cc(PAIRS, g1[:, :], g2[:, :], [g1b], [g2b])

    def gather_heads(src, srcb, g1, g1b, g2, g2b):
        for hh in range(NH):
            cc(QUADS, src[hh * 128:(hh + 1) * 128, :], g1[hh * 512:(hh + 1) * 512, :], [srcb], [g1b])
        for hh in range(NH):
            for f in range(2):
                cc(PAIRS, g1[hh * 512 + f * 256: hh * 512 + (f + 1) * 256, :],
                   g2[(hh * 2 + f) * 512:(hh * 2 + f + 1) * 512, :], [g1b], [g2b])

    seg_first = {s: tile_seq.index(s) for s in range(ns)}
    seg_cnt = {s: tile_seq.count(s) for s in range(ns)}

    def phase_A(i, first):
        j = i // 2
        A = att[j]
        with ExitStack() as es:
            ld = cx.sem("ld")
            pat = cx.sb([128, NPA], F32, "pa", es); pab = Buf()
            pwf = cx.sb([128, 4, 128], F32, "pwf", es); pwfb = Buf()
            pwt = cx.sb([128, 4, 128], BF16, "pw", es); pwb = Buf()
            hm = cx.sb([128, 16], F32, "hm", es); hmb = Buf()
            cx.sp.dma(pat[:], pa_in[j], [], [pab], ld)
            cx.sp.dma(pwf[:], pw_in[j], [], [pwfb], ld)
            cx.sp.dma(hm[:], hm_in, [], [hmb], ld)
            pab.w = {ld: ld.v}; pwfb.w = {ld: ld.v}; hmb.w = {ld: ld.v}
            cx.dve.op(lambda e: e.tensor_copy(out=pwt[:], in_=pwf[:]), [pwfb], [pwb])
            cring = Ring(cx, 1, [128, 4, T], F32, "cnt", es)
            U = cx.sb([128, 4, TP], F32, "U", es); Ub = [Buf() for _ in range(4)]
            HC = cx.sb([128, 4, TP], F32, "HC", es); HCb = [Buf() for _ in range(4)]
            Z = cx.sb([128, 4, TP], F32, "Z", es); Zb = [Buf() for _ in range(4)]
            GB = cx.sb([128, 4, TP], F32, "GB", es); GBb = [Buf() for _ in range(4)]
            pt_ = [cx.sb([128, TP], F32, "pt", es) for _ in range(3)]; ptb = [Buf() for _ in range(3)]
            dT = cx.sb([128, 4, T], BF16, "dT", es); dTb = [Buf() for _ in range(4)]
            AT = cx.sb([128, 8, T], BF16, "AT", es); ATb = [Buf() for _ in range(8)]
            qks = cx.sb([128, 16, T], BF16, "qks", es); qksb = Buf()
            vs = cx.sb([128, NH, 4, 128], BF16, "vs", es); vsb = Buf()
            eg = cx.sb([128, 8, KC * HALO], F32, "eg", es); egb = Buf()
            ea = cx.sb([128, KC, HALO], F32, "ea", es); eab = Buf()
            egs = cx.sem("egs")
            KTv = A["KTL"].ap().rearrange("(h p) n -> p h n", p=128)
            VLv = A["VL"].ap().rearrange("(h p) (c e) -> p h c e", p=128, e=128)
            GEv = GE2.ap().rearrange("(r t s p) m -> r t s p m", r=8, t=nt, s=2)
            ELv = EL.ap().rearrange("(t s p) (k m) -> t s p k m", t=nt, s=2, m=HALO)

            def gcol(base):
                return lambda kc: pat[:, base + kc: base + kc + 1]

            def wslice(wt, kc, jj):
                return wt[:, kc * 512 + jj * 128: kc * 512 + jj * 128 + 128]

            for t in range(nt):
                s = tile_seq[t]
                kseg = t - seg_first[s]
                if first:
                    cx.pool.dma(xt[:], xin[t], [], [xwb], xsem)
                else:
                    cx.pool.dma(xt[:, :, HALO:HALO + T], XL[t], [XLb[t]], [xwb], xsem)
                    for side in range(2):
                        dst = (lambda: xt[:, :, 0:HALO]) if side == 0 else (lambda: xt[:, :, HALO + T:TP])
                        nb = t - 1 if side == 0 else t + 1
                        local = (kseg > 0) if side == 0 else (kseg < seg_cnt[s] - 1)
                        if local:
                            cx.pool.dma(dst(), ELv[nb, 1 if side == 0 else 0], [ELb], [xwb], xsem)
                        else:
                            tn = seg_first[s] + seg_cnt[s] - 1 if side == 0 else seg_first[s]
                            sd = 1 if side == 0 else 0
                            cx.pool.dma(eg[:], GEv[:, tn, sd].rearrange("r p m -> p r m"), [GE2b], [egb], egs)
                            mo = 0 if side == 0 else 8
                            cx.dve.op(lambda e: e.tensor_scalar(out=ea[:].rearrange("p k m -> p (k m)"),
                                                                in0=eg[:, 0, :], scalar1=hm[:, mo:mo + 1],
                                                                scalar2=None, op0=ALU.mult), [egb, hmb], [eab])
                            for r in range(1, 8):
                                cx.dve.op(lambda e: e.scalar_tensor_tensor(
                                    out=ea[:].rearrange("p k m -> p (k m)"), in0=eg[:, r, :],
                                    scalar=hm[:, mo + r:mo + r + 1], in1=ea[:].rearrange("p k m -> p (k m)"),
                                    op0=ALU.mult, op1=ALU.add), [egb, eab], [eab])
                            cx.dve.op(lambda e: e.tensor_copy(out=dst(), in_=ea[:]), [eab], [xwb])
                ct, cb = cring.load(cx.pool, cnt[t])
                cx.dve.op(lambda e: e.reciprocal(out=ct[:], in_=ct[:]), [cb], [cb])
                rms_stats(cx, xt, xwb, 0, TP, sq, sqb, [PS[0], PS[1]], eps_t, rstd, rstdb, ones, onesb)
                apply_norm(cx, xt, xb, 0, TP, gcol(0), rstd, rstdb, h, hb)
                k = 0
                for u in range(4):
                    wt, wb = ws.get()[0]
                    for jj in range(4):
                        for (c0, cn) in ((0, 512), (512, TP - 512)):
                            pt, pb = PS[2 + (k % 4)]
                            k += 1
                            cx.mm_group(pb, [(pt[:, 0:cn], wslice(wt, kc, jj), h[:, kc, c0:c0 + cn])
                                             for kc in range(KC)], [wb] + hb)
                            if u == 0:
                                cx.act.op(lambda e: e.activation(out=U[:, jj, c0:c0 + cn], in_=pt[:, 0:cn],
                                                                 func=AF.Copy), [pb], [Ub[jj]])
                            elif u == 1:
                                cx.act.op(lambda e: e.activation(out=HC[:, jj, c0:c0 + cn], in_=pt[:, 0:cn],
                                                                 func=AF.Copy), [pb], [HCb[jj]])
                            elif u == 2:
                                cx.act.op(lambda e: e.activation(out=GB[:, jj, c0:c0 + cn], in_=pt[:, 0:cn],
                                                                 func=AF.Copy), [pb], [GBb[jj]])
                            else:
                                cx.dve.op(lambda e: e.tensor_tensor(out=Z[:, jj, c0:c0 + cn], in0=pt[:, 0:cn],
                                                                    in1=HC[:, jj, c0:c0 + cn], op=ALU.mult),
                                          [pb, HCb[jj]], [Zb[jj]])
                for g in range(4):
                    cur = (lambda a_, b_, g=g: U[:, g, a_:b_])
                    curb = Ub[g]
                    ln = TP
                    sh = 1
                    kk = 0
                    for s_ in range(g + 1):
                        ln2 = ln - sh
                        o = pt_[kk % 3]
                        ob_ = ptb[kk % 3]
                        kk += 1
                        cx.pool.op(lambda e: e.tensor_tensor(out=o[:, 0:ln2], in0=cur(0, ln2),
                                                             in1=cur(sh, sh + ln2), op=ALU.add),
                                   [curb], [ob_])
                        cur = (lambda a_, b_, o=o: o[:, a_:b_])
                        curb = ob_
                        ln = ln2
                        sh *= 2
                    w = WINDOWS[g]
                    off = HALO - w // 2
                    o = pt_[kk % 3]
                    ob_ = ptb[kk % 3]
                    cx.pool.op(lambda e: e.tensor_tensor(out=o[:, 0:T], in0=cur(off, off + T),
                                                         in1=ct[:, g, :], op=ALU.mult), [curb, cb], [ob_])
                    cx.pool.op(lambda e: e.tensor_tensor(out=dT[:, g, :], in0=o[:, 0:T],
                                                         in1=U[:, g, HALO:HALO + T], op=ALU.subtract),
                               [ob_, Ub[g]], [dTb[g]])
                    pt, pb = PS[2 + (k % 4)]
                    k += 1
                    cx.mm_group(pb, [(pt[:], pwt[:, g, :], dT[:, g, :])], [pwb, dTb[g]])
                    cx.dve.op(lambda e: e.tensor_scalar(out=AT[:, g, :], in0=pt[:],
                                                        scalar1=pat[:, 24 + g:25 + g], scalar2=None,
                                                        op0=ALU.mult), [pb, pab], [ATb[g]])
                for g in range(4):
                    o = pt_[g % 3]
                    ob_ = ptb[g % 3]
                    q = cx.dve
                    q.op(lambda e: e.tensor_scalar(out=o[:, 0:T], in0=Z[:, g, HALO - 1:HALO - 1 + T],
                                                   scalar1=pat[:, 28 + g:29 + g], scalar2=None, op0=ALU.mult),
                         [Zb[g]], [ob_])
                    q.op(lambda e: e.scalar_tensor_tensor(out=o[:, 0:T], in0=Z[:, g, HALO:HALO + T],
                                                          scalar=pat[:, 32 + g:33 + g], in1=o[:, 0:T],
                                                          op0=ALU.mult, op1=ALU.add), [Zb[g], ob_], [ob_])
                    q.op(lambda e: e.scalar_tensor_tensor(out=o[:, 0:T], in0=Z[:, g, HALO + 1:HALO + 1 + T],
                                                          scalar=pat[:, 36 + g:37 + g], in1=o[:, 0:T],
                                                          op0=ALU.mult, op1=ALU.add), [Zb[g], ob_], [ob_])
                    cx.pool.op(lambda e: e.tensor_tensor(out=AT[:, 4 + g, :], in0=o[:, 0:T],
                                                         in1=GB[:, g, HALO:HALO + T], op=ALU.mult),
                               [ob_, GBb[g]], [ATb[4 + g]])
                for u in range(2):
                    wt, wb = ws.get()[0]
                    for jj in range(4):
                        n = u * 4 + jj
                        pt, pb = PS[2 + (k % 4)]
                        k += 1
                        cx.mm_group(pb, [(pt[:], wslice(wt, c, jj), AT[:, c, :]) for c in range(8)],
                                    [wb] + ATb)
                        cx.dve.op(lambda e: e.tensor_tensor(out=xt[:, n, HALO:HALO + T],
                                                            in0=xt[:, n, HALO:HALO + T], in1=pt[:],
                                                            op=ALU.add), [pb, xwb], [xwb])
                rms_stats(cx, xt, xwb, HALO, T, sq, sqb, [PS[0]], eps_t, rstd, rstdb, ones, onesb)
                apply_norm(cx, xt, xb, HALO, T, gcol(8), rstd, rstdb, h, hb)
                mlp_block(cx, xt, xb, HALO, h, hb, ws, PS[2:5], PS[5:8], relu_t, relu_b, aT, aTb)
                cx.pool.dma(XL[t], xt[:, :, HALO:HALO + T], [xwb], [XLb[t]], osem[0])
                rms_stats(cx, xt, xwb, HALO, T, sq, sqb, [PS[0]], eps_t, rstd, rstdb, ones, onesb)
                apply_norm(cx, xt, xb, HALO, T, gcol(16), rstd, rstdb, h, hb)
                k = 0
                for u in range(4):
                    wt, wb = ws.get()[0]
                    for jj in range(4):
                        oc = u * 4 + jj
                        pt, pb = PS[2 + (k % 6)]
                        k += 1
                        cx.mm_group(pb, [(pt[:], wslice(wt, kc, jj), h[:, kc, 0:T]) for kc in range(KC)],
                                    [wb] + hb)
                        if oc % 2 == 0:
                            cx.act.op(lambda e: e.activation(out=qks[:, oc, :], in_=pt[:], func=AF.Copy),
                                      [pb], [qksb])
                        else:
                            cx.dve.op(lambda e: e.tensor_copy(out=qks[:, oc, :], in_=pt[:]), [pb], [qksb])
                cx.pool.dma(QL[t].rearrange("h p n -> p h n"), qks[:, 0:8, :], [qksb], [QLb[t]], osem[1])
                cx.pool.dma(KTv[:, :, t * T:(t + 1) * T], qks[:, 8:16, :], [qksb], [A["KTLb"]], osem[2])
                for half in range(2):
                    wt, wb = ws.get()[0]
                    for tb in range(4):
                        pt, pb = PS[2 + (k % 6)]
                        k += 1
                        cx.mm_group(pb, [(pt[:], h[:, kc, tb * 128:(tb + 1) * 128], wt[:, kc * 512:(kc + 1) * 512])
                                         for kc in range(KC)], [wb] + hb)
                        ov = vs[:, half * 4:(half + 1) * 4, tb, :]
                        iv = pt[:].rearrange("p (h e) -> p h e", e=128)
                        if tb % 2 == 0:
                            cx.act.op(lambda e: e.activation(out=ov, in_=iv, func=AF.Copy), [pb], [vsb])
                        else:
                            cx.dve.op(lambda e: e.tensor_copy(out=ov, in_=iv), [pb], [vsb])
                cx.pool.dma(VLv[:, :, t * 4:(t + 1) * 4, :], vs[:], [vsb], [A["VLb"]], osem[3])
            cx.barrier()
        gather_heads(A["KTL"], A["KTLb"], A["G1K"], A["G1Kb"], A["G2K"], A["G2Kb"])
        gather_heads(A["VL"], A["VLb"], A["G1V"], A["G1Vb"], A["G2V"], A["G2Vb"])

    def phase_B(i, final):
        j = i // 2
        A = att[j]
        G2Kv = A["G2K"].ap().rearrange("(h f q w p) n -> h f q w p n", h=NH, f=2, q=2, w=2)
        G2Vv = A["G2V"].ap().rearrange("(h f q w p) (c e) -> h f q w p c e", h=NH, f=2, q=2, w=2, e=128)
        ELv = EL.ap().rearrange("(t s p) (k m) -> t s p k m", t=nt, s=2, m=HALO)
        with ExitStack() as es:
            ld = cx.sem("ld")
            pbt = cx.sb([128, NPB], F32, "pb", es); pbb = Buf()
            dlt = cx.sb([128, ndl], F32, "dl", es); dlb = Buf()
            d0 = cx.sb([128, 1408], F32, "d0", es); d0b = Buf()
            cng = cx.sb([128, 16], F32, "cng", es); cngb = Buf()
            cx.sp.dma(pbt[:], pb_in[j], [], [pbb], ld)
            cx.sp.dma(dlt[:], dl_in, [], [dlb], ld)
            cx.sp.dma(d0[:], d0_in, [], [d0b], ld)
            cx.sp.dma(cng[:], cng_in, [], [cngb], ld)
            for b_ in (pbb, dlb, d0b, cngb):
                b_.w = {ld: ld.v}
            lt = cx.sb([128, 128], F32, "lt", es); ltb = Buf()
            l2 = cx.sb([128, 2], F32, "l2", es); l2b = Buf()
            nlam = cx.sb([128, 1], F32, "nlam", es); nlamb = Buf()
            L0 = 17
            cx.dve.op(lambda e: e.tensor_tensor(out=lt[:, 0:64], in0=pbt[:, L0:L0 + 64],
                                                in1=pbt[:, L0 + 64:L0 + 128], op=ALU.mult), [pbb], [ltb])
            cx.dve.op(lambda e: e.tensor_tensor(out=lt[:, 64:128], in0=pbt[:, L0 + 128:L0 + 192],
                                                in1=pbt[:, L0 + 192:L0 + 256], op=ALU.mult), [pbb, ltb], [ltb])
            cx.dve.op(lambda e: e.reduce_sum(out=l2[:, 0:1], in_=lt[:, 0:64], axis=mybir.AxisListType.X),
                      [ltb], [l2b])
            cx.dve.op(lambda e: e.reduce_sum(out=l2[:, 1:2], in_=lt[:, 64:128], axis=mybir.AxisListType.X),
                      [ltb, l2b], [l2b])
            cx.act.op(lambda e: e.activation(out=l2[:], in_=l2[:], func=AF.Exp), [l2b], [l2b])
            cx.dve.op(lambda e: e.tensor_tensor(out=nlam[:], in0=l2[:, 1:2], in1=l2[:, 0:1], op=ALU.subtract),
                      [l2b], [nlamb])
            cx.dve.op(lambda e: e.tensor_tensor(out=nlam[:], in0=nlam[:], in1=pbt[:, L0 + 256:L0 + 257],
                                                op=ALU.subtract), [nlamb, pbb], [nlamb])
            qring = Ring(cx, 3, [128, T], BF16, "q", es)
            qs = Stream(cx, cx.sp, [qring], [(QL[t, hd],) for t in range(nt) for hd in range(NH)])
            kring = Ring(cx, 4, [128, KB], BF16, "k", es)
            vring = Ring(cx, 4, [128, KB // 128, 128], BF16, "v", es)
            kvsrc = []
            for t in range(nt):
                s = tile_seq[t]
                per = seg_cnt[s] * T
                loc0 = seg_first[s] * T
                for hd in range(NH):
                    for m in range(2):
                        for blk in range(seq_len[s] // KB):
                            pos = blk * KB
                            r = pos // per
                            lo = loc0 + pos % per
                            kvsrc.append((r, hd, lo, m))

            class KVStream(Stream):
                def _issue(self_):
                    i_ = self_.ni
                    r, hd, lo, m = kvsrc[i_]
                    out = []
                    f_, q_, w_ = (r % 4) // 2, r // 4, r % 2
                    for ring, src, gb, psl in ((kring, G2Kv[hd, f_, q_, w_, 64 * m:64 * m + 64, lo:lo + KB], A["G2Kb"], True),
                                               (vring, G2Vv[hd, f_, q_, w_, :, lo // 128:(lo + KB) // 128, :], A["G2Vb"], False)):
                        k_ = ring.i
                        ring.i = (ring.i + 1) % ring.n
                        dst = ring.t[k_][64 * m:64 * m + 64, :] if psl else ring.t[k_][:]
                        self_.q.dma(dst, src, [gb], [ring.b[k_]], ring.s[k_])
                        out.append((ring.t[k_], ring.b[k_]))
                    self_.slots[i_] = out
                    self_.ni += 1
            kvs = KVStream(cx, cx.sp, [kring, vring], kvsrc)
            t1 = [cx.sb([128, 1408], F32, "t1", es) for _ in range(2)]; t1b = [Buf() for _ in range(2)]
            t2 = [cx.sb([128, T], F32, "t2", es) for _ in range(4)]; t2b = [Buf() for _ in range(4)]
            pT = [cx.sb([128, T], BF16, "pT", es) for _ in range(4)]; pTb = [Buf() for _ in range(4)]
            ep = [cx.sb([128, T], F32, "ep", es) for _ in range(4)]; epb = [Buf() for _ in range(4)]
            osq = cx.sb([128, T], BF16, "osq", es); osqb = Buf()
            Et = cx.sb([128, 1408], F32, "Et", es); Etb = Buf()
            OT = cx.sb([128, NH, T], BF16, "OT", es); OTb = [Buf() for _ in range(NH)]
            so_y = [cx.sem("soy") for _ in range(4)]
            cx.out_sems += so_y
            UA, UB, ZA, ZB = PS[4], PS[5], PS[6], PS[7]

            for t in range(nt):
                s = tile_seq[t]
                nch = nchunks[t]
                nblk = seq_len[s] // KB
                cx.pool.dma(xt[:, :, 0:T], XL[t], [XLb[t]], [xwb], xsem)
                cx.dve.op(lambda e: e.tensor_scalar(out=Et[:], in0=d0[:], scalar1=dlt[:, dbase[t]:dbase[t] + 1],
                                                    scalar2=None, op0=ALU.add), [d0b, dlb], [Etb])
                for hd in range(NH):
                    qt, qb = qs.get()[0]
                    cslope = -(2.0 ** (-(hd + 1))) / SCALE
                    rA, rB, oA, oB = ep
                    for m in range(2):
                        Ux, Zx = PS[4 + m], PS[6 + m]
                        lo_p = 64 * m

                        def chunk_iter():
                            for blk in range(nblk):
                                (kt_, kb__), (vt_, vb__) = kvs.get()
                                for c_ in range(KB // 128):
                                    yield (kt_, kb__, vt_, vb__, c_)

                        def stage_S(ci, kt, kb_, c):
                            st, stb = PS[ci % 4]
                            cx.pe.deps([kb_, qb], [stb])
                            i2 = nc.tensor.matmul(st[:], lhsT=kt[lo_p:lo_p + 64, c * 128:(c + 1) * 128],
                                                  rhs=qt[lo_p:lo_p + 64, :], start=True, stop=True)
                            cx.pe.finish(i2, [kb_, qb], [stb])
                            if c == 0:
                                bk = ci // 8
                                cx.act.op(lambda e: e.activation(out=t1[bk % 2][:], in_=Et[:], func=AF.Abs,
                                                                 bias=cng[:, bk:bk + 1], scale=1.0),
                                          [Etb, cngb], [t1b[bk % 2]])

                        def stage_D(ci):
                            i4 = ci % 4
                            st, stb = PS[i4]
                            g_ = (ci // 8) % 2
                            go = 128 * (7 - ci % 8)
                            cx.dve.op(lambda e: e.scalar_tensor_tensor(out=t2[i4][:], in0=t1[g_][:, go:go + T],
                                                                       scalar=cslope, in1=st[:],
                                                                       op0=ALU.mult, op1=ALU.add),
                                      [t1b[g_], stb], [t2b[i4]])
                            cx.act.op(lambda e: e.activation(out=pT[i4][:], in_=t2[i4][:], func=AF.Exp,
                                                             scale=SCALE), [t2b[i4]], [pTb[i4]])

                        def stage_P(ci, vt, vb_, c):
                            i4 = ci % 4
                            first = (ci == 0)
                            last = (ci == nch - 1)
                            cx.pe.deps([vb_, pTb[i4], one1b], [Ux[1], Zx[1]])
                            nc.tensor.matmul(Ux[0][:], lhsT=vt[:, c, :], rhs=pT[i4][:], start=first, stop=last)
                            i3 = nc.tensor.matmul(Zx[0][:], lhsT=one1[:], rhs=pT[i4][:], start=first, stop=last)
                            cx.pe.finish(i3, [vb_, pTb[i4], one1b], [Ux[1], Zx[1]] if last else [])

                        LA = 3
                        it = chunk_iter()
                        pend = []
                        for ci0 in range(min(LA, nch)):
                            cdesc = next(it)
                            pend.append(cdesc)
                            stage_S(ci0, cdesc[0], cdesc[1], cdesc[4])
                        for ci in range(nch):
                            if ci + LA < nch:
                                cdesc = next(it)
                                pend.append(cdesc)
                                stage_S(ci + LA, cdesc[0], cdesc[1], cdesc[4])
                            cur = pend.pop(0)
                            stage_D(ci)
                            stage_P(ci, cur[2], cur[3], cur[4])
                        rr, oo = (rA, oA) if m == 0 else (rB, oB)
                        rrb, oob = (epb[0], epb[2]) if m == 0 else (epb[1], epb[3])
                        cx.dve.op(lambda e: e.reciprocal(out=rr[:], in_=Zx[0][:]), [Zx[1]], [rrb])
                        cx.dve.op(lambda e: e.tensor_tensor(out=oo[:], in0=Ux[0][:], in1=rr[:], op=ALU.mult),
                                  [Ux[1], rrb], [oob])
                    cx.dve.op(lambda e: e.scalar_tensor_tensor(out=oA[:], in0=oB[:], scalar=nlam[:, 0:1],
                                                               in1=oA[:], op0=ALU.mult, op1=ALU.add),
                              [epb[3], epb[2], nlamb], [epb[2]])
                    cx.act.op(lambda e: e.activation(out=osq[:], in_=oA[:], func=AF.Square), [epb[2]], [osqb])
                    pt, pb = PS[0]
                    cx.mm_group(pb, [(pt[:], onee[:], osq[:])], [osqb, oneeb])
                    cx.act.op(lambda e: e.activation(out=rA[:], in_=pt[:], func=AF.Ln, bias=eps_t[:, 1:2],
                                                     scale=1.0), [pb, epsb], [epb[0]])
                    cx.act.op(lambda e: e.activation(out=rA[:], in_=rA[:], func=AF.Exp, scale=-0.5),
                              [epb[0]], [epb[0]])
                    cx.dve.op(lambda e: e.scalar_tensor_tensor(out=OT[:, hd, :], in0=oA[:], scalar=pbt[:, 16:17],
                                                               in1=rA[:], op0=ALU.mult, op1=ALU.mult),
                              [epb[2], epb[0], pbb], [OTb[hd]])
                k = 0
                for u in range(2):
                    wt, wb = ws.get()[0]
                    for jj in range(4):
                        n = u * 4 + jj
                        pt, pb = PS[k % 4]
                        k += 1
                        cx.mm_group(pb, [(pt[:], wt[:, c * 512 + jj * 128: c * 512 + jj * 128 + 128], OT[:, c, :])
                                         for c in range(NH)], [wb] + OTb)
                        cx.dve.op(lambda e: e.tensor_tensor(out=xt[:, n, 0:T], in0=xt[:, n, 0:T], in1=pt[:],
                                                            op=ALU.add), [pb, xwb], [xwb])
                rms_stats(cx, xt, xwb, 0, T, sq, sqb, [PS[0]], eps_t, rstd, rstdb, ones, onesb)
                apply_norm(cx, xt, xb, 0, T, lambda kc: pbt[:, kc:kc + 1], rstd, rstdb, h, hb)
                mlp_block(cx, xt, xb, 0, h, hb, ws, PS[1:4], PS[4:8], relu_t, relu_b, aT, aTb)
                if not final:
                    cx.pool.dma(XL[t], xt[:, :, 0:T], [xwb], [XLb[t]], osem[0])
                    cx.pool.dma(ELv[t, 0], xt[:, :, 0:HALO], [xwb], [ELb], osem[4])
                    cx.pool.dma(ELv[t, 1], xt[:, :, T - HALO:T], [xwb], [ELb], osem[5])
                else:
                    rms_stats(cx, xt, xwb, 0, T, sq, sqb, [PS[0]], eps_t, rstd, rstdb, ones, onesb)
                    for kc in range(KC):
                        yk, ykb = ep[kc % 4], epb[kc % 4]
                        cx.dve.op(lambda e: e.scalar_tensor_tensor(out=yk[:], in0=xt[:, kc, 0:T],
                                                                   scalar=pbt[:, 8 + kc:9 + kc], in1=rstd[:, 0:T],
                                                                   op0=ALU.mult, op1=ALU.mult),
                                  [xwb, rstdb], [ykb])
                        cx.pool.dma(yout[t, :, kc, :], yk[:], [ykb], [], so_y[kc % 4])
            cx.barrier()
        if not final:
            gather(EL, ELb, GE1, GE1b, GE2, GE2b)

    phase_A(0, True)
    phase_B(1, False)
    phase_A(2, False)
    phase_B(3, True)
    cx.finish_outputs()
    return nc


def units_k1024(w):
    n = w.shape[1]
    return np.ascontiguousarray(w.reshape(KC, 128, n // 512, 512).transpose(2, 1, 0, 3)).reshape(n // 512, 128, UNIT)


def units_w2(w):
    return np.ascontiguousarray(w.reshape(4, 8, 128, 2, 512).transpose(0, 3, 2, 1, 4)).reshape(8, 128, UNIT)


_CACHE = {}


def run_model(inp, seqs):
    ncores = 8
    ns = len(seqs)
    slen = [x.shape[0] for x in seqs]
    tiles_per = [s // ncores // T for s in slen]
    tile_seq = [s for s in range(ns) for _ in range(tiles_per[s])]
    nt = len(tile_seq)
    cores = list(range(ncores))

    ulist = []
    for i in range(4):
        j = i // 2
        if i % 2 == 0:
            ulist += list(units_k1024(inp["mix_in_w"][j])) + list(units_k1024(inp["mix_out_w"][j]))
        else:
            ulist += list(units_k1024(inp["attn_qkv_w"][j])) + list(units_k1024(inp["attn_out_w"][j]))
        ulist += list(units_k1024(inp["mlp_w1"][i])) + list(units_w2(inp["mlp_w2"][i]))
    wsrc = np.stack(ulist).astype(np.float32)
    assert wsrc.shape[0] == NU

    def col(v):
        return v.reshape(KC, 128).T

    def tile_start(c, t):
        s = tile_seq[t]
        k = t - tile_seq.index(s)
        return s, c * (slen[s] // ncores) + k * T

    xs = [np.asarray(x, np.float32) for x in seqs]
    xh = np.zeros((ncores, nt, 128, KC, TP), np.float32)
    cnts = np.zeros((ncores, nt, 128, 4, T), np.float32)
    dls = np.zeros((ncores, 128, sum(slen[tile_seq[t]] // 128 for t in range(nt))), np.float32)
    hms = np.zeros((ncores, 128, 16), np.float32)
    for c in cores:
        o = 0
        if c > 0:
            hms[c, :, c - 1] = 1.0
        if c < ncores - 1:
            hms[c, :, 8 + c + 1] = 1.0
        for t in range(nt):
            s, t0 = tile_start(c, t)
            S = slen[s]
            lo, hi = t0 - HALO, t0 + T + HALO
            a, b = max(lo, 0), min(hi, S)
            blk = np.zeros((TP, D), np.float32)
            blk[a - lo:b - lo] = xs[s][a:b]
            xh[c, t] = blk.T.reshape(KC, 128, TP).transpose(1, 0, 2)
            tt = np.arange(t0, t0 + T)
            for g, w in enumerate(WINDOWS):
                cnts[c, t, :, g, :] = (np.clip(tt + w // 2, 0, S) - np.clip(tt - w // 2, 0, S))[None, :]
            n = S // 128
            dls[c, :, o:o + n] = (t0 - 128 * np.arange(n))[None, :]
            o += n
    pa = np.zeros((2, 128, NPA), np.float32)
    pw = np.zeros((2, 128, 4, 128), np.float32)
    pb = np.zeros((2, 128, NPB), np.float32)
    for j in range(2):
        i = 2 * j
        pa[j, :, 0:8] = col(inp["norm1_g"][i])
        pa[j, :, 8:16] = col(inp["norm2_g"][i])
        pa[j, :, 16:24] = col(inp["norm1_g"][i + 1])
        pa[j, :, 24:28] = inp["pool_scale"][j].reshape(4, 128).T
        for tap in range(3):
            pa[j, :, 28 + 4 * tap:32 + 4 * tap] = inp["conv_w"][j][tap].reshape(4, 128).T
        pw[j] = inp["pool_w"][j].transpose(1, 0, 2)
        li = 0.8 - 0.6 * math.exp(-0.3 * (i + 1))
        pb[j, :, 0:8] = col(inp["norm2_g"][i + 1])
        pb[j, :, 8:16] = col(inp["final_g"])
        pb[j, :, 16] = inp["subln_g"][j] * np.float32(1.0 - li)
        pb[j, :, 17:17 + 64] = inp["lambda_q1"][j][None]
        pb[j, :, 17 + 64:17 + 128] = inp["lambda_k1"][j][None]
        pb[j, :, 17 + 128:17 + 192] = inp["lambda_q2"][j][None]
        pb[j, :, 17 + 192:17 + 256] = inp["lambda_k2"][j][None]
        pb[j, :, 17 + 256] = li
    d0np = (np.arange(1408, dtype=np.float32)[None, :] - np.arange(128, dtype=np.float32)[:, None])
    cngnp = np.tile((-128.0 * (8 * np.arange(16, dtype=np.float32) + 7))[None, :], (128, 1))
    key = ("F", tuple(tile_seq), tuple(slen))
    if key not in _CACHE:
        _CACHE[key] = build_fused(tile_seq, slen)
    ncF = _CACHE[key]
    in_maps = [{"xin": xh[c], "cnt": cnts[c], "wsrc": wsrc, "pa": pa, "pw": pw, "pb": pb, "dl": dls[c],
                "d0": d0np, "cng": cngnp, "hm": hms[c]} for c in cores]
    res = run_bass_kernel_spmd(ncF, in_maps, core_ids=cores).results
    out = [np.zeros((slen[s], D), np.float32) for s in range(ns)]
    for c in cores:
        yo = res[c]["yout"]
        for t in range(nt):
            s, t0 = tile_start(c, t)
            out[s][t0:t0 + T] = yo[t].transpose(1, 0, 2).reshape(D, T).T
    return out


def kernel(**inputs):
    inp = {k: np.asarray(v) for k, v in inputs.items()}
    xp = inp["x_prompt"]
    xsm = inp["x_sample"]
    seqs = [xp[b] for b in range(xp.shape[0])] + [xsm[b] for b in range(xsm.shape[0])]
    ys = run_model(inp, seqs)
    nb = xp.shape[0]
    return (np.stack(ys[:nb]).astype(np.float32), np.stack(ys[nb:]).astype(np.float32))
```

```python
import math
from contextlib import ExitStack
import numpy as np
import ml_dtypes
import concourse.bass as bass
import concourse.mybir as mybir
from concourse.bass_utils import run_bass_kernel_spmd

F32, BF16 = mybir.dt.float32, mybir.dt.bfloat16
AF = mybir.ActivationFunctionType
ALU = mybir.AluOpType
BF = ml_dtypes.bfloat16

D = 1024
KC = 8
T = 512
HALO = 8
TP = T + 2 * HALO
DFF = 4096
WINDOWS = (2, 4, 8, 16)
NH = 8
SCALE = 0.125
KB = 1024
UNIT = 4096


class Sem:
    def __init__(self, nc, name):
        self.h = nc.alloc_semaphore(name)
        self.v = 0


class Buf:
    def __init__(self):
        self.w = {}
        self.r = {}


class Q:
    def __init__(self, nc, eng, name):
        self.e = eng
        self.sem = Sem(nc, name)
        self.seen = {}

    def wait_tok(self, sem, v):
        if self.seen.get(sem, 0) >= v:
            return
        self.e.wait_ge(sem.h, v)
        self.seen[sem] = v

    def deps(self, reads, writes):
        for b in reads:
            for sem, v in b.w.items():
                self.wait_tok(sem, v)
        for b in writes:
            for sem, v in b.w.items():
                self.wait_tok(sem, v)
            for sem, v in b.r.items():
                self.wait_tok(sem, v)

    def finish(self, inst, reads, writes, sem=None, inc=1):
        sem = sem or self.sem
        sem.v += inc
        inst.then_inc(sem.h, inc)
        for b in reads:
            b.r[sem] = max(b.r.get(sem, 0), sem.v)
        for b in writes:
            b.w = {sem: sem.v}
            b.r = {}
        return (sem, sem.v)

    def op(self, fn, reads=(), writes=()):
        self.deps(reads, writes)
        return self.finish(fn(self.e), reads, writes)

    def dma(self, out, in_, reads, writes, sem):
        self.deps(reads, writes)
        return self.finish(self.e.dma_start(out=out, in_=in_), reads, writes, sem=sem, inc=16)


class Ctx:
    def __init__(self, nc):
        self.nc = nc
        self.pe = Q(nc, nc.tensor, "q_pe")
        self.act = Q(nc, nc.scalar, "q_act")
        self.dve = Q(nc, nc.vector, "q_dve")
        self.pool = Q(nc, nc.gpsimd, "q_pool")
        self.sp = Q(nc, nc.sync, "q_sp")
        self._n = 0
        self.out_sems = []
        self.all_sems = []

    def name(self, p):
        self._n += 1
        return f"{p}{self._n}"

    def sb(self, shape, dt, p="t", es=None):
        if es is not None:
            return es.enter_context(self.nc.sbuf_tensor(self.name(p), list(shape), dt))
        return self.nc.alloc_sbuf_tensor(self.name(p), list(shape), dt)

    def ps(self, p="ps"):
        return self.nc.alloc_psum_tensor(self.name(p), [128, 512], F32)

    def sem(self, p="s"):
        s = Sem(self.nc, self.name(p))
        self.all_sems.append(s)
        return s

    def barrier(self):
        qs = [self.pe, self.act, self.dve, self.pool, self.sp]
        for q in qs:
            for o in qs:
                if o is not q and o.sem.v > 0:
                    q.wait_tok(o.sem, o.sem.v)
            for s in self.all_sems:
                if s.v > 0:
                    q.wait_tok(s, s.v)

    def mm_group(self, psb, mms, reads):
        self.pe.deps(reads, [psb])
        n = len(mms)
        inst = None
        for i, (o, l, r) in enumerate(mms):
            inst = self.nc.tensor.matmul(o, lhsT=l, rhs=r, start=(i == 0), stop=(i == n - 1))
        return self.pe.finish(inst, reads, [psb])

    def finish_outputs(self):
        for s in self.out_sems:
            self.sp.wait_tok(s, s.v)


class Ring:
    def __init__(self, cx, n, shape, dt, p, es=None):
        self.cx = cx
        self.t = [cx.sb(shape, dt, p, es) for _ in range(n)]
        self.b = [Buf() for _ in range(n)]
        self.s = [cx.sem(p + "s") for _ in range(n)]
        self.i = 0
        self.n = n

    def load(self, q, src_ap):
        i = self.i
        self.i = (self.i + 1) % self.n
        q.dma(self.t[i][:], src_ap, [], [self.b[i]], self.s[i])
        return self.t[i], self.b[i]


class Stream:
    def __init__(self, cx, q, rings, srcs):
        self.q = q
        self.rings = rings
        self.srcs = srcs
        self.ni = 0
        self.ng = 0
        self.slots = {}

    def _issue(self):
        i = self.ni
        self.slots[i] = [r.load(self.q, a) for r, a in zip(self.rings, self.srcs[i])]
        self.ni += 1

    def get(self):
        i = self.ng
        la = self.rings[0].n - 2
        while self.ni < len(self.srcs) and self.ni <= i + la:
            self._issue()
        self.ng += 1
        return self.slots.pop(i)


def rms_stats(cx, xt, xb, lo, n, sq, sqb, ps_list, eps_t, rstd, rstdb, ones, onesb, tmpb=None):
    cx.act.op(lambda e: e.activation(out=sq[:, :, 0:n], in_=xt[:, :, lo:lo + n], func=AF.Square),
              [xb], [sqb])
    c0 = 0
    k = 0
    while c0 < n:
        cn = min(512, n - c0)
        pt, pb = ps_list[k]
        cx.mm_group(pb, [(pt[:, 0:cn], ones[:], sq[:, kc, c0:c0 + cn]) for kc in range(KC)],
                    [sqb, onesb])
        cx.act.op(lambda e: e.activation(out=rstd[:, c0:c0 + cn], in_=pt[:, 0:cn], func=AF.Ln,
                                         bias=eps_t[:, 0:1], scale=1.0), [pb], [rstdb])
        cx.act.op(lambda e: e.activation(out=rstd[:, c0:c0 + cn], in_=rstd[:, c0:c0 + cn],
                                         func=AF.Exp, scale=-0.5), [rstdb], [rstdb])
        c0 += cn
        k += 1


def mlp_block(cx, xt, xb, xlo, h2, h2b, ws, pA, pB, relu_t, relu_b, aT, aTb):
    kA = 0
    kB = 0
    for fb in range(4):
        wu = []
        for j in range(2):
            wu.append(ws.get()[0])
        for f in range(8):
            wt, wb = wu[f // 4]
            pt, pb = pA[kA % len(pA)]
            kA += 1
            cx.mm_group(pb, [(pt[:], wt[:, kc * 512 + (f % 4) * 128: kc * 512 + (f % 4) * 128 + 128],
                              h2[:, kc, 0:T]) for kc in range(KC)], [wb] + h2b)
            rt, rb = relu_t[f % len(relu_t)], relu_b[f % len(relu_t)]
            cx.act.op(lambda e: e.activation(out=rt[:], in_=pt[:], func=AF.Relu), [pb], [rb])
            cx.pool.op(lambda e: e.tensor_tensor(out=aT[:, f, :], in0=rt[:], in1=rt[:], op=ALU.mult),
                       [rb], [aTb[f]])
        wv = []
        for j in range(2):
            wv.append(ws.get()[0])
        for n in range(8):
            wt, wb = wv[n // 4]
            pt, pb = pB[kB % len(pB)]
            kB += 1
            cx.mm_group(pb, [(pt[:], wt[:, f * 512 + (n % 4) * 128: f * 512 + (n % 4) * 128 + 128],
                              aT[:, f, :]) for f in range(8)], [wb] + aTb)
            cx.dve.op(lambda e: e.tensor_tensor(out=xt[:, n, xlo:xlo + T], in0=xt[:, n, xlo:xlo + T],
                                                in1=pt[:], op=ALU.add), [pb, xb[n]], [xb[n]])


def apply_norm(cx, xt, xb, lo, n, gcol, rstd, rstdb, h, hb):
    for kc in range(KC):
        q = cx.dve
        q.op(lambda e: e.scalar_tensor_tensor(out=h[:, kc, 0:n], in0=xt[:, kc, lo:lo + n],
                                              scalar=gcol(kc), in1=rstd[:, 0:n],
                                              op0=ALU.mult, op1=ALU.mult),
             [xb[kc], rstdb], [hb[kc]])


NPA = 40
NPB = 8 + 8 + 1 + 256 + 2
NU = 92
QUADS = [[0, 1, 2, 3], [4, 5, 6, 7]]
PAIRS = [[0, 4], [1, 5], [2, 6], [3, 7]]
W_IN = {0: 0, 2: 46}
W_OUT = {0: 4, 2: 50}
W_QKV = {1: 22, 3: 68}
W_AO = {1: 28, 3: 74}
W_1 = {0: 6, 1: 30, 2: 52, 3: 76}
W_2 = {0: 14, 1: 38, 2: 60, 3: 84}


def mlp_units(i):
    return [b + 2 * fb + j for fb in range(4) for b in (W_1[i], W_2[i]) for j in (0, 1)]


def build_fused(tile_seq, seq_len):
    nt = len(tile_seq)
    ns = len(seq_len)
    NT = nt * T
    nc = bass.Bass("TRN2", target_bir_lowering=False)

    def ext(name, shape, dt, out=False):
        return nc.dram_tensor(name, list(shape), dt, kind="ExternalOutput" if out else "ExternalInput").ap()

    xin = ext("xin", [nt, 128, KC, TP], F32)
    cnt = ext("cnt", [nt, 128, 4, T], F32)
    wsrc = ext("wsrc", [NU, 128, UNIT], F32)
    pa_in = ext("pa", [2, 128, NPA], F32)
    pw_in = ext("pw", [2, 128, 4, 128], F32)
    pb_in = ext("pb", [2, 128, NPB], F32)
    nchunks = [seq_len[tile_seq[t]] // 128 for t in range(nt)]
    dbase = [sum(nchunks[:t]) for t in range(nt)]
    ndl = sum(nchunks)
    dl_in = ext("dl", [128, ndl], F32)
    d0_in = ext("d0", [128, 1408], F32)
    cng_in = ext("cng", [128, 16], F32)
    hm_in = ext("hm", [128, 16], F32)
    yout = ext("yout", [nt, 128, KC, T], F32, out=True)

    WB = nc.dram_tensor("WB", [NU, 128, UNIT], BF16).ap()
    WBb = [Buf() for _ in range(NU)]
    XL = nc.dram_tensor("XL", [nt, 128, KC, T], F32).ap()
    XLb = [Buf() for _ in range(nt)]
    QL = nc.dram_tensor("QL", [nt, NH, 128, T], BF16).ap()
    QLb = [Buf() for _ in range(nt)]
    att = []
    for a in range(2):
        d = {}
        d["KTL"] = nc.dram_tensor(f"KTL{a}", [NH * 128, NT], BF16)
        d["VL"] = nc.dram_tensor(f"VL{a}", [NH * 128, nt * 4 * 128], BF16)
        d["G1K"] = nc.dram_tensor(f"G1K{a}", [4 * NH * 128, NT], BF16)
        d["G2K"] = nc.dram_tensor(f"G2K{a}", [8 * NH * 128, NT], BF16)
        d["G1V"] = nc.dram_tensor(f"G1V{a}", [4 * NH * 128, nt * 4 * 128], BF16)
        d["G2V"] = nc.dram_tensor(f"G2V{a}", [8 * NH * 128, nt * 4 * 128], BF16)
        for k in list(d.keys()):
            d[k + "b"] = Buf()
        att.append(d)
    EL = nc.dram_tensor("EL", [nt * 2 * 128, KC * HALO], F32)
    GE1 = nc.dram_tensor("GE1", [4 * nt * 2 * 128, KC * HALO], F32)
    GE2 = nc.dram_tensor("GE2", [8 * nt * 2 * 128, KC * HALO], F32)
    ELb, GE1b, GE2b = Buf(), Buf(), Buf()

    cx = Ctx(nc)
    PS = [(cx.ps(), Buf()) for _ in range(8)]
    ones = cx.sb([128, 128], BF16, "ones"); onesb = Buf()
    cx.pool.op(lambda e: e.memset(ones[:], 1.0 / D), [], [onesb])
    one1 = cx.sb([128, 128], BF16, "one1"); one1b = Buf()
    cx.pool.op(lambda e: e.memset(one1[:], 1.0), [], [one1b])
    onee = cx.sb([128, 128], BF16, "onee"); oneeb = Buf()
    cx.pool.op(lambda e: e.memset(onee[:], 1.0 / 128), [], [oneeb])
    eps_t = cx.sb([128, 2], F32, "eps"); epsb = Buf()
    cx.pool.op(lambda e: e.memset(eps_t[:, 0:1], 1e-6), [], [epsb])
    cx.pool.op(lambda e: e.memset(eps_t[:, 1:2], 1e-5), [], [epsb])
    cx.act.deps([epsb], [])
    wring = Ring(cx, 6, [128, UNIT], BF16, "wr")
    wseq = []
    for i in range(4):
        for t in range(nt):
            if i % 2 == 0:
                wseq += [W_IN[i] + u for u in range(4)] + [W_OUT[i] + u for u in range(2)] + mlp_units(i) \
                    + [W_QKV[i + 1] + u for u in range(6)]
            else:
                wseq += [W_AO[i] + u for u in range(2)] + mlp_units(i)

    class WStream(Stream):
        def _issue(self_):
            i = self_.ni
            u = wseq[i]
            ring = self_.rings[0]
            k = ring.i
            ring.i = (ring.i + 1) % ring.n
            self_.q.dma(ring.t[k][:], WB[u], [WBb[u]], [ring.b[k]], ring.s[k])
            self_.slots[i] = [(ring.t[k], ring.b[k])]
            self_.ni += 1
    ws = WStream(cx, cx.sp, [wring], wseq)
    xt = cx.sb([128, KC, TP], F32, "x"); xwb = Buf()
    xsem = cx.sem("xs")
    sq = cx.sb([128, KC, TP], BF16, "sq"); sqb = Buf()
    rstd = cx.sb([128, TP], F32, "rstd"); rstdb = Buf()
    h = cx.sb([128, KC, TP], BF16, "h"); hb = [Buf() for _ in range(KC)]
    relu_t = [cx.sb([128, T], F32, "rl") for _ in range(3)]; relu_b = [Buf() for _ in range(3)]
    aT = cx.sb([128, 8, T], BF16, "aT"); aTb = [Buf() for _ in range(8)]
    osem = [cx.sem("os") for _ in range(6)]
    cx.out_sems += osem
    xb = [xwb] * KC

    with ExitStack() as es:
        ring = Ring(cx, 2, [128, UNIT], F32, "wst", es)
        ob = [cx.sb([128, UNIT], BF16, "wo", es) for _ in range(2)]
        obb = [Buf() for _ in range(2)]
        obs = [cx.sem("wos") for _ in range(2)]
        for u in range(NU):
            t_, b_ = ring.load(cx.sp, wsrc[u])
            k = u % 2
            if u % 2 == 0:
                cx.dve.op(lambda e: e.tensor_copy(out=ob[k][:], in_=t_[:]), [b_], [obb[k]])
            else:
                cx.act.op(lambda e: e.activation(out=ob[k][:], in_=t_[:], func=AF.Copy), [b_], [obb[k]])
            cx.pool.dma(WB[u], ob[k][:], [obb[k]], [WBb[u]], obs[k])
        cx.barrier()

    ccs = cx.sem("cc")
    ccb = Buf()

    def cc(groups, src_ap, dst_ap, reads, writes):
        cx.pool.deps(reads, writes + [ccb])
        ins = nc.gpsimd.collective_compute("AllGather", ALU.bypass, replica_groups=groups,
                                           ins=[src_ap], outs=[dst_ap])
        cx.pool.finish(ins, reads, writes + [ccb], sem=ccs, inc=1)

    def gather(src, srcb, g1, g1b, g2, g2b):
        cc(QUADS, src[:, :], g1[:, :], [srcb], [g1b])
        cc(PAIRS, g1[:, :], g2[:, :], [g1b], [g2b])

    def gather_heads(src, srcb, g1, g1b, g2, g2b):
        for hh in range(NH):
            cc(QUADS, src[hh * 128:(hh + 1) * 128, :], g1[hh * 512:(hh + 1) * 512, :], [srcb], [g1b])
        for hh in range(NH):
            for f in range(2):
                cc(PAIRS, g1[hh * 512 + f * 256: hh * 512 + (f + 1) * 256, :],
                   g2[(hh * 2 + f) * 512:(hh * 2 + f + 1) * 512, :], [g1b], [g2b])

    seg_first = {s: tile_seq.index(s) for s in range(ns)}
    seg_cnt = {s: tile_seq.count(s) for s in range(ns)}

    def phase_A(i, first):
        j = i // 2
        A = att[j]
        with ExitStack() as es:
            ld = cx.sem("ld")
            pat = cx.sb([128, NPA], F32, "pa", es); pab = Buf()
            pwf = cx.sb([128, 4, 128], F32, "pwf", es); pwfb = Buf()
            pwt = cx.sb([128, 4, 128], BF16, "pw", es); pwb = Buf()
            hm = cx.sb([128, 16], F32, "hm", es); hmb = Buf()
            cx.sp.dma(pat[:], pa_in[j], [], [pab], ld)
            cx.sp.dma(pwf[:], pw_in[j], [], [pwfb], ld)
            cx.sp.dma(hm[:], hm_in, [], [hmb], ld)
            pab.w = {ld: ld.v}; pwfb.w = {ld: ld.v}; hmb.w = {ld: ld.v}
            cx.dve.op(lambda e: e.tensor_copy(out=pwt[:], in_=pwf[:]), [pwfb], [pwb])
            cring = Ring(cx, 1, [128, 4, T], F32, "cnt", es)
            U = cx.sb([128, 4, TP], F32, "U", es); Ub = [Buf() for _ in range(4)]
            HC = cx.sb([128, 4, TP], F32, "HC", es); HCb = [Buf() for _ in range(4)]
            Z = cx.sb([128, 4, TP], F32, "Z", es); Zb = [Buf() for _ in range(4)]
            GB = cx.sb([128, 4, TP], F32, "GB", es); GBb = [Buf() for _ in range(4)]
            pt_ = [cx.sb([128, TP], F32, "pt", es) for _ in range(3)]; ptb = [Buf() for _ in range(3)]
            dT = cx.sb([128, 4, T], BF16, "dT", es); dTb = [Buf() for _ in range(4)]
            AT = cx.sb([128, 8, T], BF16, "AT", es); ATb = [Buf() for _ in range(8)]
            qks = cx.sb([128, 16, T], BF16, "qks", es); qksb = Buf()
            vs = cx.sb([128, NH, 4, 128], BF16, "vs", es); vsb = Buf()
            eg = cx.sb([128, 8, KC * HALO], F32, "eg", es); egb = Buf()
            ea = cx.sb([128, KC, HALO], F32, "ea", es); eab = Buf()
            egs = cx.sem("egs")
            KTv = A["KTL"].ap().rearrange("(h p) n -> p h n", p=128)
            VLv = A["VL"].ap().rearrange("(h p) (c e) -> p h c e", p=128, e=128)
            GEv = GE2.ap().rearrange("(r t s p) m -> r t s p m", r=8, t=nt, s=2)
            ELv = EL.ap().rearrange("(t s p) (k m) -> t s p k m", t=nt, s=2, m=HALO)

            def gcol(base):
                return lambda kc: pat[:, base + kc: base + kc + 1]

            def wslice(wt, kc, jj):
                return wt[:, kc * 512 + jj * 128: kc * 512 + jj * 128 + 128]

            for t in range(nt):
                s = tile_seq[t]
                kseg = t - seg_first[s]
                if first:
                    cx.pool.dma(xt[:], xin[t], [], [xwb], xsem)
                else:
                    cx.pool.dma(xt[:, :, HALO:HALO + T], XL[t], [XLb[t]], [xwb], xsem)
                    for side in range(2):
                        dst = (lambda: xt[:, :, 0:HALO]) if side == 0 else (lambda: xt[:, :, HALO + T:TP])
                        nb = t - 1 if side == 0 else t + 1
                        local = (kseg > 0) if side == 0 else (kseg < seg_cnt[s] - 1)
                        if local:
                            cx.pool.dma(dst(), ELv[nb, 1 if side == 0 else 0], [ELb], [xwb], xsem)
                        else:
                            tn = seg_first[s] + seg_cnt[s] - 1 if side == 0 else seg_first[s]
                            sd = 1 if side == 0 else 0
                            cx.pool.dma(eg[:], GEv[:, tn, sd].rearrange("r p m -> p r m"), [GE2b], [egb], egs)
                            mo = 0 if side == 0 else 8
                            cx.dve.op(lambda e: e.tensor_scalar(out=ea[:].rearrange("p k m -> p (k m)"),
                                                                in0=eg[:, 0, :], scalar1=hm[:, mo:mo + 1],
                                                                scalar2=None, op0=ALU.mult), [egb, hmb], [eab])
                            for r in range(1, 8):
                                cx.dve.op(lambda e: e.scalar_tensor_tensor(
                                    out=ea[:].rearrange("p k m -> p (k m)"), in0=eg[:, r, :],
                                    scalar=hm[:, mo + r:mo + r + 1], in1=ea[:].rearrange("p k m -> p (k m)"),
                                    op0=ALU.mult, op1=ALU.add), [egb, eab], [eab])
                            cx.dve.op(lambda e: e.tensor_copy(out=dst(), in_=ea[:]), [eab], [xwb])
                ct, cb = cring.load(cx.pool, cnt[t])
                cx.dve.op(lambda e: e.reciprocal(out=ct[:], in_=ct[:]), [cb], [cb])
                rms_stats(cx, xt, xwb, 0, TP, sq, sqb, [PS[0], PS[1]], eps_t, rstd, rstdb, ones, onesb)
                apply_norm(cx, xt, xb, 0, TP, gcol(0), rstd, rstdb, h, hb)
                k = 0
                for u in range(4):
                    wt, wb = ws.get()[0]
                    for jj in range(4):
                        for (c0, cn) in ((0, 512), (512, TP - 512)):
                            pt, pb = PS[2 + (k % 4)]
                            k += 1
                            cx.mm_group(pb, [(pt[:, 0:cn], wslice(wt, kc, jj), h[:, kc, c0:c0 + cn])
                                             for kc in range(KC)], [wb] + hb)
                            if u == 0:
                                cx.act.op(lambda e: e.activation(out=U[:, jj, c0:c0 + cn], in_=pt[:, 0:cn],
                                                                 func=AF.Copy), [pb], [Ub[jj]])
                            elif u == 1:
                                cx.act.op(lambda e: e.activation(out=HC[:, jj, c0:c0 + cn], in_=pt[:, 0:cn],
                                                                 func=AF.Copy), [pb], [HCb[jj]])
                            elif u == 2:
                                cx.act.op(lambda e: e.activation(out=GB[:, jj, c0:c0 + cn], in_=pt[:, 0:cn],
                                                                 func=AF.Copy), [pb], [GBb[jj]])
                            else:
                                cx.dve.op(lambda e: e.tensor_tensor(out=Z[:, jj, c0:c0 + cn], in0=pt[:, 0:cn],
                                                                    in1=HC[:, jj, c0:c0 + cn], op=ALU.mult),
                                          [pb, HCb[jj]], [Zb[jj]])
                for g in range(4):
                    cur = (lambda a_, b_, g=g: U[:, g, a_:b_])
                    curb = Ub[g]
                    ln = TP
                    sh = 1
                    kk = 0
                    for s_ in range(g + 1):
                        ln2 = ln - sh
                        o = pt_[kk % 3]
                        ob_ = ptb[kk % 3]
                        kk += 1
                        cx.pool.op(lambda e: e.tensor_tensor(out=o[:, 0:ln2], in0=cur(0, ln2),
                                                             in1=cur(sh, sh + ln2), op=ALU.add),
                                   [curb], [ob_])
                        cur = (lambda a_, b_, o=o: o[:, a_:b_])
                        curb = ob_
                        ln = ln2
                        sh *= 2
                    w = WINDOWS[g]
                    off = HALO - w // 2
                    o = pt_[kk % 3]
                    ob_ = ptb[kk % 3]
                    cx.pool.op(lambda e: e.tensor_tensor(out=o[:, 0:T], in0=cur(off, off + T),
                                                         in1=ct[:, g, :], op=ALU.mult), [curb, cb], [ob_])
                    cx.pool.op(lambda e: e.tensor_tensor(out=dT[:, g, :], in0=o[:, 0:T],
                                                         in1=U[:, g, HALO:HALO + T], op=ALU.subtract),
                               [ob_, Ub[g]], [dTb[g]])
                    pt, pb = PS[2 + (k % 4)]
                    k += 1
                    cx.mm_group(pb, [(pt[:], pwt[:, g, :], dT[:, g, :])], [pwb, dTb[g]])
                    cx.dve.op(lambda e: e.tensor_scalar(out=AT[:, g, :], in0=pt[:],
                                                        scalar1=pat[:, 24 + g:25 + g], scalar2=None,
                                                        op0=ALU.mult), [pb, pab], [ATb[g]])
                for g in range(4):
                    o = pt_[g % 3]
                    ob_ = ptb[g % 3]
                    q = cx.dve
                    q.op(lambda e: e.tensor_scalar(out=o[:, 0:T], in0=Z[:, g, HALO - 1:HALO - 1 + T],
                                                   scalar1=pat[:, 28 + g:29 + g], scalar2=None, op0=ALU.mult),
                         [Zb[g]], [ob_])
                    q.op(lambda e: e.scalar_tensor_tensor(out=o[:, 0:T], in0=Z[:, g, HALO:HALO + T],
                                                          scalar=pat[:, 32 + g:33 + g], in1=o[:, 0:T],
                                                          op0=ALU.mult, op1=ALU.add), [Zb[g], ob_], [ob_])
                    q.op(lambda e: e.scalar_tensor_tensor(out=o[:, 0:T], in0=Z[:, g, HALO + 1:HALO + 1 + T],
                                                          scalar=pat[:, 36 + g:37 + g], in1=o[:, 0:T],
                                                          op0=ALU.mult, op1=ALU.add), [Zb[g], ob_], [ob_])
                    cx.pool.op(lambda e: e.tensor_tensor(out=AT[:, 4 + g, :], in0=o[:, 0:T],
                                                         in1=GB[:, g, HALO:HALO + T], op=ALU.mult),
                               [ob_, GBb[g]], [ATb[4 + g]])
                for u in range(2):
                    wt, wb = ws.get()[0]
                    for jj in range(4):
                        n = u * 4 + jj
                        pt, pb = PS[2 + (k % 4)]
                        k += 1
                        cx.mm_group(pb, [(pt[:], wslice(wt, c, jj), AT[:, c, :]) for c in range(8)],
                                    [wb] + ATb)
                        cx.dve.op(lambda e: e.tensor_tensor(out=xt[:, n, HALO:HALO + T],
                                                            in0=xt[:, n, HALO:HALO + T], in1=pt[:],
                                                            op=ALU.add), [pb, xwb], [xwb])
                rms_stats(cx, xt, xwb, HALO, T, sq, sqb, [PS[0]], eps_t, rstd, rstdb, ones, onesb)
                apply_norm(cx, xt, xb, HALO, T, gcol(8), rstd, rstdb, h, hb)
                mlp_block(cx, xt, xb, HALO, h, hb, ws, PS[2:5], PS[5:8], relu_t, relu_b, aT, aTb)
                cx.pool.dma(XL[t], xt[:, :, HALO:HALO + T], [xwb], [XLb[t]], osem[0])
                rms_stats(cx, xt, xwb, HALO, T, sq, sqb, [PS[0]], eps_t, rstd, rstdb, ones, onesb)
                apply_norm(cx, xt, xb, HALO, T, gcol(16), rstd, rstdb, h, hb)
                k = 0
                for u in range(4):
                    wt, wb = ws.get()[0]
                    for jj in range(4):
                        oc = u * 4 + jj
                        pt, pb = PS[2 + (k % 6)]
                        k += 1
                        cx.mm_group(pb, [(pt[:], wslice(wt, kc, jj), h[:, kc, 0:T]) for kc in range(KC)],
                                    [wb] + hb)
                        if oc % 2 == 0:
                            cx.act.op(lambda e: e.activation(out=qks[:, oc, :], in_=pt[:], func=AF.Copy),
                                      [pb], [qksb])
                        else:
                            cx.dve.op(lambda e: e.tensor_copy(out=qks[:, oc, :], in_=pt[:]), [pb], [qksb])
                cx.pool.dma(QL[t].rearrange("h p n -> p h n"), qks[:, 0:8, :], [qksb], [QLb[t]], osem[1])
                cx.pool.dma(KTv[:, :, t * T:(t + 1) * T], qks[:, 8:16, :], [qksb], [A["KTLb"]], osem[2])
                for half in range(2):
                    wt, wb = ws.get()[0]
                    for tb in range(4):
                        pt, pb = PS[2 + (k % 6)]
                        k += 1
                        cx.mm_group(pb, [(pt[:], h[:, kc, tb * 128:(tb + 1) * 128], wt[:, kc * 512:(kc + 1) * 512])
                                         for kc in range(KC)], [wb] + hb)
                        ov = vs[:, half * 4:(half + 1) * 4, tb, :]
                        iv = pt[:].rearrange("p (h e) -> p h e", e=128)
                        if tb % 2 == 0:
                            cx.act.op(lambda e: e.activation(out=ov, in_=iv, func=AF.Copy), [pb], [vsb])
                        else:
                            cx.dve.op(lambda e: e.tensor_copy(out=ov, in_=iv), [pb], [vsb])
                cx.pool.dma(VLv[:, :, t * 4:(t + 1) * 4, :], vs[:], [vsb], [A["VLb"]], osem[3])
            cx.barrier()
        gather_heads(A["KTL"], A["KTLb"], A["G1K"], A["G1Kb"], A["G2K"], A["G2Kb"])
        gather_heads(A["VL"], A["VLb"], A["G1V"], A["G1Vb"], A["G2V"], A["G2Vb"])

    def phase_B(i, final):
        j = i // 2
        A = att[j]
        G2Kv = A["G2K"].ap().rearrange("(h f q w p) n -> h f q w p n", h=NH, f=2, q=2, w=2)
        G2Vv = A["G2V"].ap().rearrange("(h f q w p) (c e) -> h f q w p c e", h=NH, f=2, q=2, w=2, e=128)
        ELv = EL.ap().rearrange("(t s p) (k m) -> t s p k m", t=nt, s=2, m=HALO)
        with ExitStack() as es:
            ld = cx.sem("ld")
            pbt = cx.sb([128, NPB], F32, "pb", es); pbb = Buf()
            dlt = cx.sb([128, ndl], F32, "dl", es); dlb = Buf()
            d0 = cx.sb([128, 1408], F32, "d0", es); d0b = Buf()
            cng = cx.sb([128, 16], F32, "cng", es); cngb = Buf()
            cx.sp.dma(pbt[:], pb_in[j], [], [pbb], ld)
            cx.sp.dma(dlt[:], dl_in, [], [dlb], ld)
            cx.sp.dma(d0[:], d0_in, [], [d0b], ld)
            cx.sp.dma(cng[:], cng_in, [], [cngb], ld)
            for b_ in (pbb, dlb, d0b, cngb):
                b_.w = {ld: ld.v}
            lt = cx.sb([128, 128], F32, "lt", es); ltb = Buf()
            l2 = cx.sb([128, 2], F32, "l2", es); l2b = Buf()
            nlam = cx.sb([128, 1], F32, "nlam", es); nlamb = Buf()
            L0 = 17
            cx.dve.op(lambda e: e.tensor_tensor(out=lt[:, 0:64], in0=pbt[:, L0:L0 + 64],
                                                in1=pbt[:, L0 + 64:L0 + 128], op=ALU.mult), [pbb], [ltb])
            cx.dve.op(lambda e: e.tensor_tensor(out=lt[:, 64:128], in0=pbt[:, L0 + 128:L0 + 192],
                                                in1=pbt[:, L0 + 192:L0 + 256], op=ALU.mult), [pbb, ltb], [ltb])
            cx.dve.op(lambda e: e.reduce_sum(out=l2[:, 0:1], in_=lt[:, 0:64], axis=mybir.AxisListType.X),
                      [ltb], [l2b])
            cx.dve.op(lambda e: e.reduce_sum(out=l2[:, 1:2], in_=lt[:, 64:128], axis=mybir.AxisListType.X),
                      [ltb, l2b], [l2b])
            cx.act.op(lambda e: e.activation(out=l2[:], in_=l2[:], func=AF.Exp), [l2b], [l2b])
            cx.dve.op(lambda e: e.tensor_tensor(out=nlam[:], in0=l2[:, 1:2], in1=l2[:, 0:1], op=ALU.subtract),
                      [l2b], [nlamb])
            cx.dve.op(lambda e: e.tensor_tensor(out=nlam[:], in0=nlam[:], in1=pbt[:, L0 + 256:L0 + 257],
                                                op=ALU.subtract), [nlamb, pbb], [nlamb])
            qring = Ring(cx, 3, [128, T], BF16, "q", es)
            qs = Stream(cx, cx.sp, [qring], [(QL[t, hd],) for t in range(nt) for hd in range(NH)])
            kring = Ring(cx, 4, [128, KB], BF16, "k", es)
            vring = Ring(cx, 4, [128, KB // 128, 128], BF16, "v", es)
            kvsrc = []
            for t in range(nt):
                s = tile_seq[t]
                per = seg_cnt[s] * T
                loc0 = seg_first[s] * T
                for hd in range(NH):
                    for blk in range(seq_len[s] // KB):
                        pos = blk * KB
                        r = pos // per
                        lo = loc0 + pos % per
                        kvsrc.append((r, hd, lo))

            class KVStream(Stream):
                def _issue(self_):
                    i_ = self_.ni
                    r, hd, lo = kvsrc[i_]
                    out = []
                    f_, q_, w_ = (r % 4) // 2, r // 4, r % 2
                    for ring, src, gb in ((kring, G2Kv[hd, f_, q_, w_, :, lo:lo + KB], A["G2Kb"]),
                                          (vring, G2Vv[hd, f_, q_, w_, :, lo // 128:(lo + KB) // 128, :], A["G2Vb"])):
                        k_ = ring.i
                        ring.i = (ring.i + 1) % ring.n
                        self_.q.dma(ring.t[k_][:], src, [gb], [ring.b[k_]], ring.s[k_])
                        out.append((ring.t[k_], ring.b[k_]))
                    self_.slots[i_] = out
                    self_.ni += 1
            kvs = KVStream(cx, cx.sp, [kring, vring], kvsrc)
            t1 = [cx.sb([128, 1408], F32, "t1", es) for _ in range(2)]; t1b = [Buf() for _ in range(2)]
            t2 = [cx.sb([128, T], F32, "t2", es) for _ in range(6)]; t2b = [Buf() for _ in range(6)]
            pT = [cx.sb([128, T], BF16, "pT", es) for _ in range(6)]; pTb = [Buf() for _ in range(6)]
            ep = [cx.sb([128, T], F32, "ep", es) for _ in range(4)]; epb = [Buf() for _ in range(4)]
            osq = cx.sb([128, T], BF16, "osq", es); osqb = Buf()
            Et = cx.sb([128, 1408], F32, "Et", es); Etb = Buf()
            OT = cx.sb([128, NH, T], BF16, "OT", es); OTb = [Buf() for _ in range(NH)]
            so_y = [cx.sem("soy") for _ in range(4)]
            cx.out_sems += so_y
            UA, UB, ZA, ZB = PS[4], PS[5], PS[6], PS[7]

            for t in range(nt):
                s = tile_seq[t]
                nch = nchunks[t]
                nblk = seq_len[s] // KB
                cx.pool.dma(xt[:, :, 0:T], XL[t], [XLb[t]], [xwb], xsem)
                cx.dve.op(lambda e: e.tensor_scalar(out=Et[:], in0=d0[:], scalar1=dlt[:, dbase[t]:dbase[t] + 1],
                                                    scalar2=None, op0=ALU.add), [d0b, dlb], [Etb])
                for hd in range(NH):
                    qt, qb = qs.get()[0]
                    cslope = -(2.0 ** (-(hd + 1))) / SCALE
                    def chunk_iter():
                        for blk in range(nblk):
                            (kt_, kb__), (vt_, vb__) = kvs.get()
                            for c_ in range(KB // 128):
                                yield (kt_, kb__, vt_, vb__, c_)

                    def stage_S(ci, kt, kb_, c):
                        a = ci % 2
                        sA, sAb = PS[2 * a]
                        sB, sBb = PS[2 * a + 1]
                        cx.pe.deps([kb_, qb], [sAb, sBb])
                        nc.tensor.matmul(sA[:], lhsT=kt[0:64, c * 128:(c + 1) * 128], rhs=qt[0:64, :],
                                         start=True, stop=True)
                        i2 = nc.tensor.matmul(sB[:], lhsT=kt[64:128, c * 128:(c + 1) * 128],
                                              rhs=qt[64:128, :], start=True, stop=True)
                        cx.pe.finish(i2, [kb_, qb], [sAb, sBb])
                        if c == 0:
                            bk = ci // 8
                            cx.act.op(lambda e: e.activation(out=t1[bk % 2][:], in_=Et[:], func=AF.Abs,
                                                             bias=cng[:, bk:bk + 1], scale=1.0),
                                      [Etb, cngb], [t1b[bk % 2]])

                    def stage_D(ci):
                        a = ci % 2
                        for m in range(2):
                            st, stb = PS[2 * a + m]
                            i4 = 2 * (ci % 3) + m
                            g_ = (ci // 8) % 2
                            go = 128 * (7 - ci % 8)
                            cx.dve.op(lambda e: e.scalar_tensor_tensor(out=t2[i4][:], in0=t1[g_][:, go:go + T],
                                                                       scalar=cslope, in1=st[:],
                                                                       op0=ALU.mult, op1=ALU.add),
                                      [t1b[g_], stb], [t2b[i4]])
                            cx.act.op(lambda e: e.activation(out=pT[i4][:], in_=t2[i4][:], func=AF.Exp,
                                                             scale=SCALE), [t2b[i4]], [pTb[i4]])

                    def stage_P(ci, vt, vb_, c):
                        a = ci % 2
                        first = (ci == 0)
                        last = (ci == nch - 1)
                        for m, (Ux, Zx) in enumerate(((UA, ZA), (UB, ZB))):
                            i4 = 2 * (ci % 3) + m
                            cx.pe.deps([vb_, pTb[i4], one1b], [Ux[1], Zx[1]])
                            nc.tensor.matmul(Ux[0][:], lhsT=vt[:, c, :], rhs=pT[i4][:], start=first, stop=last)
                            i3 = nc.tensor.matmul(Zx[0][:], lhsT=one1[:], rhs=pT[i4][:], start=first,
                                                  stop=last)
                            cx.pe.finish(i3, [vb_, pTb[i4], one1b], [Ux[1], Zx[1]] if last else [])

                    LA = 2
                    it = chunk_iter()
                    pend = []
                    for ci0 in range(min(LA, nch)):
                        cd = next(it)
                        pend.append(cd)
                        stage_S(ci0, cd[0], cd[1], cd[4])
                    for ci in range(nch):
                        stage_D(ci)
                        if ci + LA < nch:
                            cd = next(it)
                            pend.append(cd)
                            stage_S(ci + LA, cd[0], cd[1], cd[4])
                        cur = pend.pop(0)
                        stage_P(ci, cur[2], cur[3], cur[4])
                    rA, rB, oA, oB = ep
                    cx.dve.op(lambda e: e.reciprocal(out=rA[:], in_=ZA[0][:]), [ZA[1]], [epb[0]])
                    cx.dve.op(lambda e: e.reciprocal(out=rB[:], in_=ZB[0][:]), [ZB[1]], [epb[1]])
                    cx.dve.op(lambda e: e.tensor_tensor(out=oA[:], in0=UA[0][:], in1=rA[:], op=ALU.mult),
                              [UA[1], epb[0]], [epb[2]])
                    cx.dve.op(lambda e: e.tensor_tensor(out=oB[:], in0=UB[0][:], in1=rB[:], op=ALU.mult),
                              [UB[1], epb[1]], [epb[3]])
                    cx.dve.op(lambda e: e.scalar_tensor_tensor(out=oA[:], in0=oB[:], scalar=nlam[:, 0:1],
                                                               in1=oA[:], op0=ALU.mult, op1=ALU.add),
                              [epb[3], epb[2], nlamb], [epb[2]])
                    cx.act.op(lambda e: e.activation(out=osq[:], in_=oA[:], func=AF.Square), [epb[2]], [osqb])
                    pt, pb = PS[0]
                    cx.mm_group(pb, [(pt[:], onee[:], osq[:])], [osqb, oneeb])
                    cx.act.op(lambda e: e.activation(out=rA[:], in_=pt[:], func=AF.Ln, bias=eps_t[:, 1:2],
                                                     scale=1.0), [pb, epsb], [epb[0]])
                    cx.act.op(lambda e: e.activation(out=rA[:], in_=rA[:], func=AF.Exp, scale=-0.5),
                              [epb[0]], [epb[0]])
                    cx.dve.op(lambda e: e.scalar_tensor_tensor(out=OT[:, hd, :], in0=oA[:], scalar=pbt[:, 16:17],
                                                               in1=rA[:], op0=ALU.mult, op1=ALU.mult),
                              [epb[2], epb[0], pbb], [OTb[hd]])
                k = 0
                for u in range(2):
                    wt, wb = ws.get()[0]
                    for jj in range(4):
                        n = u * 4 + jj
                        pt, pb = PS[k % 4]
                        k += 1
                        cx.mm_group(pb, [(pt[:], wt[:, c * 512 + jj * 128: c * 512 + jj * 128 + 128], OT[:, c, :])
                                         for c in range(NH)], [wb] + OTb)
                        cx.dve.op(lambda e: e.tensor_tensor(out=xt[:, n, 0:T], in0=xt[:, n, 0:T], in1=pt[:],
                                                            op=ALU.add), [pb, xwb], [xwb])
                rms_stats(cx, xt, xwb, 0, T, sq, sqb, [PS[0]], eps_t, rstd, rstdb, ones, onesb)
                apply_norm(cx, xt, xb, 0, T, lambda kc: pbt[:, kc:kc + 1], rstd, rstdb, h, hb)
                mlp_block(cx, xt, xb, 0, h, hb, ws, PS[1:4], PS[4:8], relu_t, relu_b, aT, aTb)
                if not final:
                    cx.pool.dma(XL[t], xt[:, :, 0:T], [xwb], [XLb[t]], osem[0])
                    cx.pool.dma(ELv[t, 0], xt[:, :, 0:HALO], [xwb], [ELb], osem[4])
                    cx.pool.dma(ELv[t, 1], xt[:, :, T - HALO:T], [xwb], [ELb], osem[5])
                else:
                    rms_stats(cx, xt, xwb, 0, T, sq, sqb, [PS[0]], eps_t, rstd, rstdb, ones, onesb)
                    for kc in range(KC):
                        yk, ykb = ep[kc % 4], epb[kc % 4]
                        cx.dve.op(lambda e: e.scalar_tensor_tensor(out=yk[:], in0=xt[:, kc, 0:T],
                                                                   scalar=pbt[:, 8 + kc:9 + kc], in1=rstd[:, 0:T],
                                                                   op0=ALU.mult, op1=ALU.mult),
                                  [xwb, rstdb], [ykb])
                        cx.pool.dma(yout[t, :, kc, :], yk[:], [ykb], [], so_y[kc % 4])
            cx.barrier()
        if not final:
            gather(EL, ELb, GE1, GE1b, GE2, GE2b)

    phase_A(0, True)
    phase_B(1, False)
    phase_A(2, False)
    phase_B(3, True)
    cx.finish_outputs()
    return nc


def units_k1024(w):
    n = w.shape[1]
    return np.ascontiguousarray(w.reshape(KC, 128, n // 512, 512).transpose(2, 1, 0, 3)).reshape(n // 512, 128, UNIT)


def units_w2(w):
    return np.ascontiguousarray(w.reshape(4, 8, 128, 2, 512).transpose(0, 3, 2, 1, 4)).reshape(8, 128, UNIT)


_CACHE = {}


def run_model(inp, seqs):
    ncores = 8
    ns = len(seqs)
    slen = [x.shape[0] for x in seqs]
    tiles_per = [s // ncores // T for s in slen]
    tile_seq = [s for s in range(ns) for _ in range(tiles_per[s])]
    nt = len(tile_seq)
    cores = list(range(ncores))

    ulist = []
    for i in range(4):
        j = i // 2
        if i % 2 == 0:
            ulist += list(units_k1024(inp["mix_in_w"][j])) + list(units_k1024(inp["mix_out_w"][j]))
        else:
            ulist += list(units_k1024(inp["attn_qkv_w"][j])) + list(units_k1024(inp["attn_out_w"][j]))
        ulist += list(units_k1024(inp["mlp_w1"][i])) + list(units_w2(inp["mlp_w2"][i]))
    wsrc = np.stack(ulist).astype(np.float32)
    assert wsrc.shape[0] == NU

    def col(v):
        return v.reshape(KC, 128).T

    def tile_start(c, t):
        s = tile_seq[t]
        k = t - tile_seq.index(s)
        return s, c * (slen[s] // ncores) + k * T

    xs = [np.asarray(x, np.float32) for x in seqs]
    xh = np.zeros((ncores, nt, 128, KC, TP), np.float32)
    cnts = np.zeros((ncores, nt, 128, 4, T), np.float32)
    dls = np.zeros((ncores, 128, sum(slen[tile_seq[t]] // 128 for t in range(nt))), np.float32)
    hms = np.zeros((ncores, 128, 16), np.float32)
    for c in cores:
        o = 0
        if c > 0:
            hms[c, :, c - 1] = 1.0
        if c < ncores - 1:
            hms[c, :, 8 + c + 1] = 1.0
        for t in range(nt):
            s, t0 = tile_start(c, t)
            S = slen[s]
            lo, hi = t0 - HALO, t0 + T + HALO
            a, b = max(lo, 0), min(hi, S)
            blk = np.zeros((TP, D), np.float32)
            blk[a - lo:b - lo] = xs[s][a:b]
            xh[c, t] = blk.T.reshape(KC, 128, TP).transpose(1, 0, 2)
            tt = np.arange(t0, t0 + T)
            for g, w in enumerate(WINDOWS):
                cnts[c, t, :, g, :] = (np.clip(tt + w // 2, 0, S) - np.clip(tt - w // 2, 0, S))[None, :]
            n = S // 128
            dls[c, :, o:o + n] = (t0 - 128 * np.arange(n))[None, :]
            o += n
    pa = np.zeros((2, 128, NPA), np.float32)
    pw = np.zeros((2, 128, 4, 128), np.float32)
    pb = np.zeros((2, 128, NPB), np.float32)
    for j in range(2):
        i = 2 * j
        pa[j, :, 0:8] = col(inp["norm1_g"][i])
        pa[j, :, 8:16] = col(inp["norm2_g"][i])
        pa[j, :, 16:24] = col(inp["norm1_g"][i + 1])
        pa[j, :, 24:28] = inp["pool_scale"][j].reshape(4, 128).T
        for tap in range(3):
            pa[j, :, 28 + 4 * tap:32 + 4 * tap] = inp["conv_w"][j][tap].reshape(4, 128).T
        pw[j] = inp["pool_w"][j].transpose(1, 0, 2)
        li = 0.8 - 0.6 * math.exp(-0.3 * (i + 1))
        pb[j, :, 0:8] = col(inp["norm2_g"][i + 1])
        pb[j, :, 8:16] = col(inp["final_g"])
        pb[j, :, 16] = inp["subln_g"][j] * np.float32(1.0 - li)
        pb[j, :, 17:17 + 64] = inp["lambda_q1"][j][None]
        pb[j, :, 17 + 64:17 + 128] = inp["lambda_k1"][j][None]
        pb[j, :, 17 + 128:17 + 192] = inp["lambda_q2"][j][None]
        pb[j, :, 17 + 192:17 + 256] = inp["lambda_k2"][j][None]
        pb[j, :, 17 + 256] = li
    d0np = (np.arange(1408, dtype=np.float32)[None, :] - np.arange(128, dtype=np.float32)[:, None])
    cngnp = np.tile((-128.0 * (8 * np.arange(16, dtype=np.float32) + 7))[None, :], (128, 1))
    key = ("F", tuple(tile_seq), tuple(slen))
    if key not in _CACHE:
        _CACHE[key] = build_fused(tile_seq, slen)
    ncF = _CACHE[key]
    in_maps = [{"xin": xh[c], "cnt": cnts[c], "wsrc": wsrc, "pa": pa, "pw": pw, "pb": pb, "dl": dls[c],
                "d0": d0np, "cng": cngnp, "hm": hms[c]} for c in cores]
    res = run_bass_kernel_spmd(ncF, in_maps, core_ids=cores).results
    out = [np.zeros((slen[s], D), np.float32) for s in range(ns)]
    for c in cores:
        yo = res[c]["yout"]
        for t in range(nt):
            s, t0 = tile_start(c, t)
            out[s][t0:t0 + T] = yo[t].transpose(1, 0, 2).reshape(D, T).T
    return out


def kernel(**inputs):
    inp = {k: np.asarray(v) for k, v in inputs.items()}
    xp = inp["x_prompt"]
    xsm = inp["x_sample"]
    seqs = [xp[b] for b in range(xp.shape[0])] + [xsm[b] for b in range(xsm.shape[0])]
    ys = run_model(inp, seqs)
    nb = xp.shape[0]
    return (np.stack(ys[:nb]).astype(np.float32), np.stack(ys[nb:]).astype(np.float32))
```

```python
import math
from contextlib import ExitStack
import numpy as np
import ml_dtypes
import concourse.bass as bass
import concourse.mybir as mybir
from concourse.bass_utils import run_bass_kernel_spmd

F32, BF16 = mybir.dt.float32, mybir.dt.bfloat16
AF = mybir.ActivationFunctionType
ALU = mybir.AluOpType
BF = ml_dtypes.bfloat16

D = 1024
KC = 8
T = 512
HALO = 8
TP = T + 2 * HALO
DFF = 4096
WINDOWS = (2, 4, 8, 16)
NH = 8
SCALE = 0.125
KB = 1024
UNIT = 4096


class Sem:
    def __init__(self, nc, name):
        self.h = nc.alloc_semaphore(name)
        self.v = 0


class Buf:
    def __init__(self):
        self.w = {}
        self.r = {}


class Q:
    def __init__(self, nc, eng, name):
        self.e = eng
        self.sem = Sem(nc, name)
        self.seen = {}

    def wait_tok(self, sem, v):
        if self.seen.get(sem, 0) >= v:
            return
        self.e.wait_ge(sem.h, v)
        self.seen[sem] = v

    def deps(self, reads, writes):
        for b in reads:
            for sem, v in b.w.items():
                self.wait_tok(sem, v)
        for b in writes:
            for sem, v in b.w.items():
                self.wait_tok(sem, v)
            for sem, v in b.r.items():
                self.wait_tok(sem, v)

    def finish(self, inst, reads, writes, sem=None, inc=1):
        sem = sem or self.sem
        sem.v += inc
        inst.then_inc(sem.h, inc)
        for b in reads:
            b.r[sem] = max(b.r.get(sem, 0), sem.v)
        for b in writes:
            b.w = {sem: sem.v}
            b.r = {}
        return (sem, sem.v)

    def op(self, fn, reads=(), writes=()):
        self.deps(reads, writes)
        return self.finish(fn(self.e), reads, writes)

    def dma(self, out, in_, reads, writes, sem):
        self.deps(reads, writes)
        return self.finish(self.e.dma_start(out=out, in_=in_), reads, writes, sem=sem, inc=16)


class Ctx:
    def __init__(self, nc):
        self.nc = nc
        self.pe = Q(nc, nc.tensor, "q_pe")
        self.act = Q(nc, nc.scalar, "q_act")
        self.dve = Q(nc, nc.vector, "q_dve")
        self.pool = Q(nc, nc.gpsimd, "q_pool")
        self.sp = Q(nc, nc.sync, "q_sp")
        self._n = 0
        self.out_sems = []
        self.all_sems = []

    def name(self, p):
        self._n += 1
        return f"{p}{self._n}"

    def sb(self, shape, dt, p="t", es=None):
        if es is not None:
            return es.enter_context(self.nc.sbuf_tensor(self.name(p), list(shape), dt))
        return self.nc.alloc_sbuf_tensor(self.name(p), list(shape), dt)

    def ps(self, p="ps"):
        return self.nc.alloc_psum_tensor(self.name(p), [128, 512], F32)

    def sem(self, p="s"):
        s = Sem(self.nc, self.name(p))
        self.all_sems.append(s)
        return s

    def barrier(self):
        qs = [self.pe, self.act, self.dve, self.pool, self.sp]
        for q in qs:
            for o in qs:
                if o is not q and o.sem.v > 0:
                    q.wait_tok(o.sem, o.sem.v)
            for s in self.all_sems:
                if s.v > 0:
                    q.wait_tok(s, s.v)

    def mm_group(self, psb, mms, reads):
        self.pe.deps(reads, [psb])
        n = len(mms)
        inst = None
        for i, (o, l, r) in enumerate(mms):
            inst = self.nc.tensor.matmul(o, lhsT=l, rhs=r, start=(i == 0), stop=(i == n - 1))
        return self.pe.finish(inst, reads, [psb])

    def finish_outputs(self):
        for s in self.out_sems:
            self.sp.wait_tok(s, s.v)


class Ring:
    def __init__(self, cx, n, shape, dt, p, es=None):
        self.cx = cx
        self.t = [cx.sb(shape, dt, p, es) for _ in range(n)]
        self.b = [Buf() for _ in range(n)]
        self.s = [cx.sem(p + "s") for _ in range(n)]
        self.i = 0
        self.n = n

    def load(self, q, src_ap):
        i = self.i
        self.i = (self.i + 1) % self.n
        q.dma(self.t[i][:], src_ap, [], [self.b[i]], self.s[i])
        return self.t[i], self.b[i]


class Stream:
    def __init__(self, cx, q, rings, srcs):
        self.q = q
        self.rings = rings
        self.srcs = srcs
        self.ni = 0
        self.ng = 0
        self.slots = {}

    def _issue(self):
        i = self.ni
        self.slots[i] = [r.load(self.q, a) for r, a in zip(self.rings, self.srcs[i])]
        self.ni += 1

    def get(self):
        i = self.ng
        la = self.rings[0].n - 2
        while self.ni < len(self.srcs) and self.ni <= i + la:
            self._issue()
        self.ng += 1
        return self.slots.pop(i)


def rms_stats(cx, xt, xb, lo, n, sq, sqb, ps_list, eps_t, rstd, rstdb, ones, onesb, tmpb=None):
    cx.act.op(lambda e: e.activation(out=sq[:, :, 0:n], in_=xt[:, :, lo:lo + n], func=AF.Square),
              [xb], [sqb])
    c0 = 0
    k = 0
    while c0 < n:
        cn = min(512, n - c0)
        pt, pb = ps_list[k]
        cx.mm_group(pb, [(pt[:, 0:cn], ones[:], sq[:, kc, c0:c0 + cn]) for kc in range(KC)],
                    [sqb, onesb])
        cx.act.op(lambda e: e.activation(out=rstd[:, c0:c0 + cn], in_=pt[:, 0:cn], func=AF.Ln,
                                         bias=eps_t[:, 0:1], scale=1.0), [pb], [rstdb])
        cx.act.op(lambda e: e.activation(out=rstd[:, c0:c0 + cn], in_=rstd[:, c0:c0 + cn],
                                         func=AF.Exp, scale=-0.5), [rstdb], [rstdb])
        c0 += cn
        k += 1


def mlp_block(cx, xt, xb, xlo, h2, h2b, ws, pA, pB, relu_t, relu_b, aT, aTb):
    kA = 0
    kB = 0
    for fb in range(4):
        wu = []
        for j in range(2):
            wu.append(ws.get()[0])
        for f in range(8):
            wt, wb = wu[f // 4]
            pt, pb = pA[kA % len(pA)]
            kA += 1
            cx.mm_group(pb, [(pt[:], wt[:, kc * 512 + (f % 4) * 128: kc * 512 + (f % 4) * 128 + 128],
                              h2[:, kc, 0:T]) for kc in range(KC)], [wb] + h2b)
            rt, rb = relu_t[f % len(relu_t)], relu_b[f % len(relu_t)]
            cx.act.op(lambda e: e.activation(out=rt[:], in_=pt[:], func=AF.Relu), [pb], [rb])
            cx.pool.op(lambda e: e.tensor_tensor(out=aT[:, f, :], in0=rt[:], in1=rt[:], op=ALU.mult),
                       [rb], [aTb[f]])
        wv = []
        for j in range(2):
            wv.append(ws.get()[0])
        for n in range(8):
            wt, wb = wv[n // 4]
            pt, pb = pB[kB % len(pB)]
            kB += 1
            cx.mm_group(pb, [(pt[:], wt[:, f * 512 + (n % 4) * 128: f * 512 + (n % 4) * 128 + 128],
                              aT[:, f, :]) for f in range(8)], [wb] + aTb)
            cx.dve.op(lambda e: e.tensor_tensor(out=xt[:, n, xlo:xlo + T], in0=xt[:, n, xlo:xlo + T],
                                                in1=pt[:], op=ALU.add), [pb, xb[n]], [xb[n]])


def apply_norm(cx, xt, xb, lo, n, gcol, rstd, rstdb, h, hb):
    for kc in range(KC):
        q = cx.dve
        q.op(lambda e: e.scalar_tensor_tensor(out=h[:, kc, 0:n], in0=xt[:, kc, lo:lo + n],
                                              scalar=gcol(kc), in1=rstd[:, 0:n],
                                              op0=ALU.mult, op1=ALU.mult),
             [xb[kc], rstdb], [hb[kc]])


NPA = 40
NPB = 8 + 8 + 1 + 256 + 2
NU = 92
QUADS = [[0, 1, 2, 3], [4, 5, 6, 7]]
PAIRS = [[0, 4], [1, 5], [2, 6], [3, 7]]
W_IN = {0: 0, 2: 46}
W_OUT = {0: 4, 2: 50}
W_QKV = {1: 22, 3: 68}
W_AO = {1: 28, 3: 74}
W_1 = {0: 6, 1: 30, 2: 52, 3: 76}
W_2 = {0: 14, 1: 38, 2: 60, 3: 84}


def mlp_units(i):
    return [b + 2 * fb + j for fb in range(4) for b in (W_1[i], W_2[i]) for j in (0, 1)]


def build_fused(tile_seq, seq_len):
    nt = len(tile_seq)
    ns = len(seq_len)
    NT = nt * T
    nc = bass.Bass("TRN2", target_bir_lowering=False)

    def ext(name, shape, dt, out=False):
        return nc.dram_tensor(name, list(shape), dt, kind="ExternalOutput" if out else "ExternalInput").ap()

    xin = ext("xin", [nt, 128, KC, TP], F32)
    cnt = ext("cnt", [nt, 128, 4, T], F32)
    wsrc = ext("wsrc", [NU, 128, UNIT], F32)
    pa_in = ext("pa", [2, 128, NPA], F32)
    pw_in = ext("pw", [2, 128, 4, 128], F32)
    pb_in = ext("pb", [2, 128, NPB], F32)
    nchunks = [seq_len[tile_seq[t]] // 128 for t in range(nt)]
    dbase = [sum(nchunks[:t]) for t in range(nt)]
    ndl = sum(nchunks)
    dl_in = ext("dl", [128, ndl], F32)
    d0_in = ext("d0", [128, 1408], F32)
    cng_in = ext("cng", [128, 16], F32)
    hm_in = ext("hm", [128, 16], F32)
    yout = ext("yout", [nt, 128, KC, T], F32, out=True)

    WB = nc.dram_tensor("WB", [NU, 128, UNIT], BF16).ap()
    WBb = [Buf() for _ in range(NU)]
    XL = nc.dram_tensor("XL", [nt, 128, KC, T], F32).ap()
    XLb = [Buf() for _ in range(nt)]
    QL = nc.dram_tensor("QL", [nt, NH, 128, T], BF16).ap()
    QLb = [Buf() for _ in range(nt)]
    att = []
    for a in range(2):
        d = {}
        d["KTL"] = nc.dram_tensor(f"KTL{a}", [NH * 128, NT], BF16)
        d["VL"] = nc.dram_tensor(f"VL{a}", [NH * 128, nt * 4 * 128], BF16)
        d["G1K"] = nc.dram_tensor(f"G1K{a}", [4 * NH * 128, NT], BF16)
        d["G2K"] = nc.dram_tensor(f"G2K{a}", [8 * NH * 128, NT], BF16)
        d["G1V"] = nc.dram_tensor(f"G1V{a}", [4 * NH * 128, nt * 4 * 128], BF16)
        d["G2V"] = nc.dram_tensor(f"G2V{a}", [8 * NH * 128, nt * 4 * 128], BF16)
        for k in list(d.keys()):
            d[k + "b"] = Buf()
        for k in ("G1K", "G2K", "G1V", "G2V"):
            d[k + "h"] = [Buf() for _ in range(NH)]
        att.append(d)
    EL = nc.dram_tensor("EL", [nt * 2 * 128, KC * HALO], F32)
    GE1 = nc.dram_tensor("GE1", [4 * nt * 2 * 128, KC * HALO], F32)
    GE2 = nc.dram_tensor("GE2", [8 * nt * 2 * 128, KC * HALO], F32)
    ELb, GE1b, GE2b = Buf(), Buf(), Buf()

    cx = Ctx(nc)
    PS = [(cx.ps(), Buf()) for _ in range(8)]
    ones = cx.sb([128, 128], BF16, "ones"); onesb = Buf()
    cx.pool.op(lambda e: e.memset(ones[:], 1.0 / D), [], [onesb])
    one1 = cx.sb([128, 128], BF16, "one1"); one1b = Buf()
    cx.pool.op(lambda e: e.memset(one1[:], 1.0), [], [one1b])
    onee = cx.sb([128, 128], BF16, "onee"); oneeb = Buf()
    cx.pool.op(lambda e: e.memset(onee[:], 1.0 / 128), [], [oneeb])
    eps_t = cx.sb([128, 2], F32, "eps"); epsb = Buf()
    cx.pool.op(lambda e: e.memset(eps_t[:, 0:1], 1e-6), [], [epsb])
    cx.pool.op(lambda e: e.memset(eps_t[:, 1:2], 1e-5), [], [epsb])
    cx.act.deps([epsb], [])
    wring = Ring(cx, 6, [128, UNIT], BF16, "wr")
    wseq = []
    for i in range(4):
        for t in range(nt):
            if i % 2 == 0:
                wseq += [W_IN[i] + u for u in range(4)] + [W_OUT[i] + u for u in range(2)] + mlp_units(i) \
                    + [W_QKV[i + 1] + u for u in range(6)]
            else:
                wseq += [W_AO[i] + u for u in range(2)] + mlp_units(i)

    class WStream(Stream):
        def _issue(self_):
            i = self_.ni
            u = wseq[i]
            ring = self_.rings[0]
            k = ring.i
            ring.i = (ring.i + 1) % ring.n
            self_.q.dma(ring.t[k][:], WB[u], [WBb[u]], [ring.b[k]], ring.s[k])
            self_.slots[i] = [(ring.t[k], ring.b[k])]
            self_.ni += 1
    ws = WStream(cx, cx.sp, [wring], wseq)
    xt = cx.sb([128, KC, TP], F32, "x"); xwb = Buf()
    xsem = cx.sem("xs")
    sq = cx.sb([128, KC, TP], BF16, "sq"); sqb = Buf()
    rstd = cx.sb([128, TP], F32, "rstd"); rstdb = Buf()
    h = cx.sb([128, KC, TP], BF16, "h"); hb = [Buf() for _ in range(KC)]
    relu_t = [cx.sb([128, T], F32, "rl") for _ in range(3)]; relu_b = [Buf() for _ in range(3)]
    aT = cx.sb([128, 8, T], BF16, "aT"); aTb = [Buf() for _ in range(8)]
    osem = [cx.sem("os") for _ in range(6)]
    cx.out_sems += osem
    xb = [xwb] * KC

    with ExitStack() as es:
        ring = Ring(cx, 2, [128, UNIT], F32, "wst", es)
        ob = [cx.sb([128, UNIT], BF16, "wo", es) for _ in range(2)]
        obb = [Buf() for _ in range(2)]
        obs = [cx.sem("wos") for _ in range(2)]
        for u in range(NU):
            t_, b_ = ring.load(cx.sp, wsrc[u])
            k = u % 2
            if u % 2 == 0:
                cx.dve.op(lambda e: e.tensor_copy(out=ob[k][:], in_=t_[:]), [b_], [obb[k]])
            else:
                cx.act.op(lambda e: e.activation(out=ob[k][:], in_=t_[:], func=AF.Copy), [b_], [obb[k]])
            cx.pool.dma(WB[u], ob[k][:], [obb[k]], [WBb[u]], obs[k])
        cx.barrier()

    ccs = cx.sem("cc")
    ccb = Buf()

    def cc(groups, src_ap, dst_ap, reads, writes):
        cx.pool.deps(reads, writes + [ccb])
        ins = nc.gpsimd.collective_compute("AllGather", ALU.bypass, replica_groups=groups,
                                           ins=[src_ap], outs=[dst_ap])
        cx.pool.finish(ins, reads, writes + [ccb], sem=ccs, inc=1)

    def gather(src, srcb, g1, g1b, g2, g2b):
        cc(QUADS, src[:, :], g1[:, :], [srcb], [g1b])
        cc(PAIRS, g1[:, :], g2[:, :], [g1b], [g2b])

    def gather_kv(A):
        for hh in range(NH):
            for nm in ("K", "V"):
                src, srcb = (A["KTL"], A["KTLb"]) if nm == "K" else (A["VL"], A["VLb"])
                g1, g1h = A["G1" + nm], A["G1" + nm + "h"]
                cc(QUADS, src[hh * 128:(hh + 1) * 128, :], g1[hh * 512:(hh + 1) * 512, :], [srcb], [g1h[hh]])
            for nm in ("K", "V"):
                g1, g1h = A["G1" + nm], A["G1" + nm + "h"]
                g2, g2h = A["G2" + nm], A["G2" + nm + "h"]
                for f in range(2):
                    cc(PAIRS, g1[hh * 512 + f * 256: hh * 512 + (f + 1) * 256, :],
                       g2[(hh * 2 + f) * 512:(hh * 2 + f + 1) * 512, :], [g1h[hh]], [g2h[hh]])

    seg_first = {s: tile_seq.index(s) for s in range(ns)}
    seg_cnt = {s: tile_seq.count(s) for s in range(ns)}

    def phase_A(i, first):
        j = i // 2
        A = att[j]
        with ExitStack() as es:
            ld = cx.sem("ld")
            pat = cx.sb([128, NPA], F32, "pa", es); pab = Buf()
            pwf = cx.sb([128, 4, 128], F32, "pwf", es); pwfb = Buf()
            pwt = cx.sb([128, 4, 128], BF16, "pw", es); pwb = Buf()
            hm = cx.sb([128, 16], F32, "hm", es); hmb = Buf()
            cx.sp.dma(pat[:], pa_in[j], [], [pab], ld)
            cx.sp.dma(pwf[:], pw_in[j], [], [pwfb], ld)
            cx.sp.dma(hm[:], hm_in, [], [hmb], ld)
            pab.w = {ld: ld.v}; pwfb.w = {ld: ld.v}; hmb.w = {ld: ld.v}
            cx.dve.op(lambda e: e.tensor_copy(out=pwt[:], in_=pwf[:]), [pwfb], [pwb])
            cring = Ring(cx, 1, [128, 4, T], F32, "cnt", es)
            U = cx.sb([128, 4, TP], F32, "U", es); Ub = [Buf() for _ in range(4)]
            HC = cx.sb([128, 4, TP], F32, "HC", es); HCb = [Buf() for _ in range(4)]
            Z = cx.sb([128, 4, TP], F32, "Z", es); Zb = [Buf() for _ in range(4)]
            GB = cx.sb([128, 4, TP], F32, "GB", es); GBb = [Buf() for _ in range(4)]
            pt_ = [cx.sb([128, TP], F32, "pt", es) for _ in range(3)]; ptb = [Buf() for _ in range(3)]
            dT = cx.sb([128, 4, T], BF16, "dT", es); dTb = [Buf() for _ in range(4)]
            AT = cx.sb([128, 8, T], BF16, "AT", es); ATb = [Buf() for _ in range(8)]
            qks = cx.sb([128, 16, T], BF16, "qks", es); qksb = Buf()
            vs = cx.sb([128, NH, 4, 128], BF16, "vs", es); vsb = Buf()
            eg = cx.sb([128, 8, KC * HALO], F32, "eg", es); egb = Buf()
            ea = cx.sb([128, KC, HALO], F32, "ea", es); eab = Buf()
            egs = cx.sem("egs")
            KTv = A["KTL"].ap().rearrange("(h p) n -> p h n", p=128)
            VLv = A["VL"].ap().rearrange("(h p) (c e) -> p h c e", p=128, e=128)
            GEv = GE2.ap().rearrange("(r t s p) m -> r t s p m", r=8, t=nt, s=2)
            ELv = EL.ap().rearrange("(t s p) (k m) -> t s p k m", t=nt, s=2, m=HALO)

            def gcol(base):
                return lambda kc: pat[:, base + kc: base + kc + 1]

            def wslice(wt, kc, jj):
                return wt[:, kc * 512 + jj * 128: kc * 512 + jj * 128 + 128]

            for t in range(nt):
                s = tile_seq[t]
                kseg = t - seg_first[s]
                if first:
                    cx.pool.dma(xt[:], xin[t], [], [xwb], xsem)
                else:
                    cx.pool.dma(xt[:, :, HALO:HALO + T], XL[t], [XLb[t]], [xwb], xsem)
                    for side in range(2):
                        dst = (lambda: xt[:, :, 0:HALO]) if side == 0 else (lambda: xt[:, :, HALO + T:TP])
                        nb = t - 1 if side == 0 else t + 1
                        local = (kseg > 0) if side == 0 else (kseg < seg_cnt[s] - 1)
                        if local:
                            cx.pool.dma(dst(), ELv[nb, 1 if side == 0 else 0], [ELb], [xwb], xsem)
                        else:
                            tn = seg_first[s] + seg_cnt[s] - 1 if side == 0 else seg_first[s]
                            sd = 1 if side == 0 else 0
                            cx.pool.dma(eg[:], GEv[:, tn, sd].rearrange("r p m -> p r m"), [GE2b], [egb], egs)
                            mo = 0 if side == 0 else 8
                            cx.dve.op(lambda e: e.tensor_scalar(out=ea[:].rearrange("p k m -> p (k m)"),
                                                                in0=eg[:, 0, :], scalar1=hm[:, mo:mo + 1],
                                                                scalar2=None, op0=ALU.mult), [egb, hmb], [eab])
                            for r in range(1, 8):
                                cx.dve.op(lambda e: e.scalar_tensor_tensor(
                                    out=ea[:].rearrange("p k m -> p (k m)"), in0=eg[:, r, :],
                                    scalar=hm[:, mo + r:mo + r + 1], in1=ea[:].rearrange("p k m -> p (k m)"),
                                    op0=ALU.mult, op1=ALU.add), [egb, eab], [eab])
                            cx.dve.op(lambda e: e.tensor_copy(out=dst(), in_=ea[:]), [eab], [xwb])
                ct, cb = cring.load(cx.pool, cnt[t])
                cx.dve.op(lambda e: e.reciprocal(out=ct[:], in_=ct[:]), [cb], [cb])
                rms_stats(cx, xt, xwb, 0, TP, sq, sqb, [PS[0], PS[1]], eps_t, rstd, rstdb, ones, onesb)
                apply_norm(cx, xt, xb, 0, TP, gcol(0), rstd, rstdb, h, hb)
                k = 0
                for u in range(4):
                    wt, wb = ws.get()[0]
                    for jj in range(4):
                        for (c0, cn) in ((0, 512), (512, TP - 512)):
                            pt, pb = PS[2 + (k % 4)]
                            k += 1
                            cx.mm_group(pb, [(pt[:, 0:cn], wslice(wt, kc, jj), h[:, kc, c0:c0 + cn])
                                             for kc in range(KC)], [wb] + hb)
                            if u == 0:
                                cx.act.op(lambda e: e.activation(out=U[:, jj, c0:c0 + cn], in_=pt[:, 0:cn],
                                                                 func=AF.Copy), [pb], [Ub[jj]])
                            elif u == 1:
                                cx.act.op(lambda e: e.activation(out=HC[:, jj, c0:c0 + cn], in_=pt[:, 0:cn],
                                                                 func=AF.Copy), [pb], [HCb[jj]])
                            elif u == 2:
                                cx.act.op(lambda e: e.activation(out=GB[:, jj, c0:c0 + cn], in_=pt[:, 0:cn],
                                                                 func=AF.Copy), [pb], [GBb[jj]])
                            else:
                                cx.dve.op(lambda e: e.tensor_tensor(out=Z[:, jj, c0:c0 + cn], in0=pt[:, 0:cn],
                                                                    in1=HC[:, jj, c0:c0 + cn], op=ALU.mult),
                                          [pb, HCb[jj]], [Zb[jj]])
                for g in range(4):
                    cur = (lambda a_, b_, g=g: U[:, g, a_:b_])
                    curb = Ub[g]
                    ln = TP
                    sh = 1
                    kk = 0
                    for s_ in range(g + 1):
                        ln2 = ln - sh
                        o = pt_[kk % 3]
                        ob_ = ptb[kk % 3]
                        kk += 1
                        cx.pool.op(lambda e: e.tensor_tensor(out=o[:, 0:ln2], in0=cur(0, ln2),
                                                             in1=cur(sh, sh + ln2), op=ALU.add),
                                   [curb], [ob_])
                        cur = (lambda a_, b_, o=o: o[:, a_:b_])
                        curb = ob_
                        ln = ln2
                        sh *= 2
                    w = WINDOWS[g]
                    off = HALO - w // 2
                    o = pt_[kk % 3]
                    ob_ = ptb[kk % 3]
                    cx.pool.op(lambda e: e.tensor_tensor(out=o[:, 0:T], in0=cur(off, off + T),
                                                         in1=ct[:, g, :], op=ALU.mult), [curb, cb], [ob_])
                    cx.pool.op(lambda e: e.tensor_tensor(out=dT[:, g, :], in0=o[:, 0:T],
                                                         in1=U[:, g, HALO:HALO + T], op=ALU.subtract),
                               [ob_, Ub[g]], [dTb[g]])
                    pt, pb = PS[2 + (k % 4)]
                    k += 1
                    cx.mm_group(pb, [(pt[:], pwt[:, g, :], dT[:, g, :])], [pwb, dTb[g]])
                    cx.dve.op(lambda e: e.tensor_scalar(out=AT[:, g, :], in0=pt[:],
                                                        scalar1=pat[:, 24 + g:25 + g], scalar2=None,
                                                        op0=ALU.mult), [pb, pab], [ATb[g]])
                for g in range(4):
                    o = pt_[g % 3]
                    ob_ = ptb[g % 3]
                    q = cx.dve
                    q.op(lambda e: e.tensor_scalar(out=o[:, 0:T], in0=Z[:, g, HALO - 1:HALO - 1 + T],
                                                   scalar1=pat[:, 28 + g:29 + g], scalar2=None, op0=ALU.mult),
                         [Zb[g]], [ob_])
                    q.op(lambda e: e.scalar_tensor_tensor(out=o[:, 0:T], in0=Z[:, g, HALO:HALO + T],
                                                          scalar=pat[:, 32 + g:33 + g], in1=o[:, 0:T],
                                                          op0=ALU.mult, op1=ALU.add), [Zb[g], ob_], [ob_])
                    q.op(lambda e: e.scalar_tensor_tensor(out=o[:, 0:T], in0=Z[:, g, HALO + 1:HALO + 1 + T],
                                                          scalar=pat[:, 36 + g:37 + g], in1=o[:, 0:T],
                                                          op0=ALU.mult, op1=ALU.add), [Zb[g], ob_], [ob_])
                    cx.pool.op(lambda e: e.tensor_tensor(out=AT[:, 4 + g, :], in0=o[:, 0:T],
                                                         in1=GB[:, g, HALO:HALO + T], op=ALU.mult),
                               [ob_, GBb[g]], [ATb[4 + g]])
                for u in range(2):
                    wt, wb = ws.get()[0]
                    for jj in range(4):
                        n = u * 4 + jj
                        pt, pb = PS[2 + (k % 4)]
                        k += 1
                        cx.mm_group(pb, [(pt[:], wslice(wt, c, jj), AT[:, c, :]) for c in range(8)],
                                    [wb] + ATb)
                        cx.dve.op(lambda e: e.tensor_tensor(out=xt[:, n, HALO:HALO + T],
                                                            in0=xt[:, n, HALO:HALO + T], in1=pt[:],
                                                            op=ALU.add), [pb, xwb], [xwb])
                rms_stats(cx, xt, xwb, HALO, T, sq, sqb, [PS[0]], eps_t, rstd, rstdb, ones, onesb)
                apply_norm(cx, xt, xb, HALO, T, gcol(8), rstd, rstdb, h, hb)
                mlp_block(cx, xt, xb, HALO, h, hb, ws, PS[2:5], PS[5:8], relu_t, relu_b, aT, aTb)
                cx.pool.dma(XL[t], xt[:, :, HALO:HALO + T], [xwb], [XLb[t]], osem[0])
                rms_stats(cx, xt, xwb, HALO, T, sq, sqb, [PS[0]], eps_t, rstd, rstdb, ones, onesb)
                apply_norm(cx, xt, xb, HALO, T, gcol(16), rstd, rstdb, h, hb)
                k = 0
                for u in range(4):
                    wt, wb = ws.get()[0]
                    for jj in range(4):
                        oc = u * 4 + jj
                        pt, pb = PS[2 + (k % 6)]
                        k += 1
                        cx.mm_group(pb, [(pt[:], wslice(wt, kc, jj), h[:, kc, 0:T]) for kc in range(KC)],
                                    [wb] + hb)
                        if oc % 2 == 0:
                            cx.act.op(lambda e: e.activation(out=qks[:, oc, :], in_=pt[:], func=AF.Copy),
                                      [pb], [qksb])
                        else:
                            cx.dve.op(lambda e: e.tensor_copy(out=qks[:, oc, :], in_=pt[:]), [pb], [qksb])
                cx.pool.dma(QL[t].rearrange("h p n -> p h n"), qks[:, 0:8, :], [qksb], [QLb[t]], osem[1])
                cx.pool.dma(KTv[:, :, t * T:(t + 1) * T], qks[:, 8:16, :], [qksb], [A["KTLb"]], osem[2])
                for half in range(2):
                    wt, wb = ws.get()[0]
                    for tb in range(4):
                        pt, pb = PS[2 + (k % 6)]
                        k += 1
                        cx.mm_group(pb, [(pt[:], h[:, kc, tb * 128:(tb + 1) * 128], wt[:, kc * 512:(kc + 1) * 512])
                                         for kc in range(KC)], [wb] + hb)
                        ov = vs[:, half * 4:(half + 1) * 4, tb, :]
                        iv = pt[:].rearrange("p (h e) -> p h e", e=128)
                        if tb % 2 == 0:
                            cx.act.op(lambda e: e.activation(out=ov, in_=iv, func=AF.Copy), [pb], [vsb])
                        else:
                            cx.dve.op(lambda e: e.tensor_copy(out=ov, in_=iv), [pb], [vsb])
                cx.pool.dma(VLv[:, :, t * 4:(t + 1) * 4, :], vs[:], [vsb], [A["VLb"]], osem[3])
            cx.barrier()
        gather_kv(A)

    def phase_B(i, final):
        j = i // 2
        A = att[j]
        G2Kv = A["G2K"].ap().rearrange("(h f q w p) n -> h f q w p n", h=NH, f=2, q=2, w=2)
        G2Vv = A["G2V"].ap().rearrange("(h f q w p) (c e) -> h f q w p c e", h=NH, f=2, q=2, w=2, e=128)
        ELv = EL.ap().rearrange("(t s p) (k m) -> t s p k m", t=nt, s=2, m=HALO)
        with ExitStack() as es:
            ld = cx.sem("ld")
            pbt = cx.sb([128, NPB], F32, "pb", es); pbb = Buf()
            dlt = cx.sb([128, ndl], F32, "dl", es); dlb = Buf()
            d0 = cx.sb([128, 1408], F32, "d0", es); d0b = Buf()
            cng = cx.sb([128, 16], F32, "cng", es); cngb = Buf()
            cx.sp.dma(pbt[:], pb_in[j], [], [pbb], ld)
            cx.sp.dma(dlt[:], dl_in, [], [dlb], ld)
            cx.sp.dma(d0[:], d0_in, [], [d0b], ld)
            cx.sp.dma(cng[:], cng_in, [], [cngb], ld)
            for b_ in (pbb, dlb, d0b, cngb):
                b_.w = {ld: ld.v}
            lt = cx.sb([128, 128], F32, "lt", es); ltb = Buf()
            l2 = cx.sb([128, 2], F32, "l2", es); l2b = Buf()
            nlam = cx.sb([128, 1], F32, "nlam", es); nlamb = Buf()
            L0 = 17
            cx.dve.op(lambda e: e.tensor_tensor(out=lt[:, 0:64], in0=pbt[:, L0:L0 + 64],
                                                in1=pbt[:, L0 + 64:L0 + 128], op=ALU.mult), [pbb], [ltb])
            cx.dve.op(lambda e: e.tensor_tensor(out=lt[:, 64:128], in0=pbt[:, L0 + 128:L0 + 192],
                                                in1=pbt[:, L0 + 192:L0 + 256], op=ALU.mult), [pbb, ltb], [ltb])
            cx.dve.op(lambda e: e.reduce_sum(out=l2[:, 0:1], in_=lt[:, 0:64], axis=mybir.AxisListType.X),
                      [ltb], [l2b])
            cx.dve.op(lambda e: e.reduce_sum(out=l2[:, 1:2], in_=lt[:, 64:128], axis=mybir.AxisListType.X),
                      [ltb, l2b], [l2b])
            cx.act.op(lambda e: e.activation(out=l2[:], in_=l2[:], func=AF.Exp), [l2b], [l2b])
            cx.dve.op(lambda e: e.tensor_tensor(out=nlam[:], in0=l2[:, 1:2], in1=l2[:, 0:1], op=ALU.subtract),
                      [l2b], [nlamb])
            cx.dve.op(lambda e: e.tensor_tensor(out=nlam[:], in0=nlam[:], in1=pbt[:, L0 + 256:L0 + 257],
                                                op=ALU.subtract), [nlamb, pbb], [nlamb])
            qring = Ring(cx, 3, [128, T], BF16, "q", es)
            qs = Stream(cx, cx.sp, [qring], [(QL[t, hd],) for t in range(nt) for hd in range(NH)])
            kring = Ring(cx, 4, [128, KB], BF16, "k", es)
            vring = Ring(cx, 4, [128, KB // 128, 128], BF16, "v", es)
            kvsrc = []
            for t in range(nt):
                s = tile_seq[t]
                per = seg_cnt[s] * T
                loc0 = seg_first[s] * T
                for hd in range(NH):
                    for blk in range(seq_len[s] // KB):
                        pos = blk * KB
                        r = pos // per
                        lo = loc0 + pos % per
                        kvsrc.append((r, hd, lo))

            class KVStream(Stream):
                def _issue(self_):
                    i_ = self_.ni
                    r, hd, lo = kvsrc[i_]
                    out = []
                    f_, q_, w_ = (r % 4) // 2, r // 4, r % 2
                    for ring, src, gb in ((kring, G2Kv[hd, f_, q_, w_, :, lo:lo + KB], A["G2Kh"][hd]),
                                          (vring, G2Vv[hd, f_, q_, w_, :, lo // 128:(lo + KB) // 128, :], A["G2Vh"][hd])):
                        k_ = ring.i
                        ring.i = (ring.i + 1) % ring.n
                        self_.q.dma(ring.t[k_][:], src, [gb], [ring.b[k_]], ring.s[k_])
                        out.append((ring.t[k_], ring.b[k_]))
                    self_.slots[i_] = out
                    self_.ni += 1
            kvs = KVStream(cx, cx.sp, [kring, vring], kvsrc)
            t1 = [cx.sb([128, 1408], F32, "t1", es) for _ in range(2)]; t1b = [Buf() for _ in range(2)]
            t2 = [cx.sb([128, T], F32, "t2", es) for _ in range(6)]; t2b = [Buf() for _ in range(6)]
            pT = [cx.sb([128, T], BF16, "pT", es) for _ in range(6)]; pTb = [Buf() for _ in range(6)]
            ep = [cx.sb([128, T], F32, "ep", es) for _ in range(4)]; epb = [Buf() for _ in range(4)]
            osq = cx.sb([128, T], BF16, "osq", es); osqb = Buf()
            Et = cx.sb([128, 1408], F32, "Et", es); Etb = Buf()
            OT = cx.sb([128, NH, T], BF16, "OT", es); OTb = [Buf() for _ in range(NH)]
            so_y = [cx.sem("soy") for _ in range(4)]
            cx.out_sems += so_y
            UA, UB, ZA, ZB = PS[4], PS[5], PS[6], PS[7]

            for t in range(nt):
                s = tile_seq[t]
                nch = nchunks[t]
                nblk = seq_len[s] // KB
                cx.sp.dma(xt[:, :, 0:T], XL[t], [XLb[t]], [xwb], xsem)
                cx.dve.op(lambda e: e.tensor_scalar(out=Et[:], in0=d0[:], scalar1=dlt[:, dbase[t]:dbase[t] + 1],
                                                    scalar2=None, op0=ALU.add), [d0b, dlb], [Etb])
                for hd in range(NH):
                    qt, qb = qs.get()[0]
                    cslope = -(2.0 ** (-(hd + 1))) / SCALE
                    def chunk_iter():
                        for blk in range(nblk):
                            (kt_, kb__), (vt_, vb__) = kvs.get()
                            for c_ in range(KB // 128):
                                yield (kt_, kb__, vt_, vb__, c_)

                    def stage_S(ci, kt, kb_, c):
                        a = ci % 2
                        sA, sAb = PS[2 * a]
                        sB, sBb = PS[2 * a + 1]
                        cx.pe.deps([kb_, qb], [sAb, sBb])
                        nc.tensor.matmul(sA[:], lhsT=kt[0:64, c * 128:(c + 1) * 128], rhs=qt[0:64, :],
                                         start=True, stop=True)
                        i2 = nc.tensor.matmul(sB[:], lhsT=kt[64:128, c * 128:(c + 1) * 128],
                                              rhs=qt[64:128, :], start=True, stop=True)
                        cx.pe.finish(i2, [kb_, qb], [sAb, sBb])
                        if c == 0:
                            bk = ci // 8
                            cx.act.op(lambda e: e.activation(out=t1[bk % 2][:], in_=Et[:], func=AF.Abs,
                                                             bias=cng[:, bk:bk + 1], scale=1.0),
                                      [Etb, cngb], [t1b[bk % 2]])

                    def stage_D(ci):
                        a = ci % 2
                        for m in range(2):
                            st, stb = PS[2 * a + m]
                            i4 = 2 * (ci % 3) + m
                            g_ = (ci // 8) % 2
                            go = 128 * (7 - ci % 8)
                            cx.dve.op(lambda e: e.scalar_tensor_tensor(out=t2[i4][:], in0=t1[g_][:, go:go + T],
                                                                       scalar=cslope, in1=st[:],
                                                                       op0=ALU.mult, op1=ALU.add),
                                      [t1b[g_], stb], [t2b[i4]])
                            cx.act.op(lambda e: e.activation(out=pT[i4][:], in_=t2[i4][:], func=AF.Exp,
                                                             scale=SCALE), [t2b[i4]], [pTb[i4]])

                    def stage_P(ci, vt, vb_, c):
                        a = ci % 2
                        first = (ci == 0)
                        last = (ci == nch - 1)
                        for m, (Ux, Zx) in enumerate(((UA, ZA), (UB, ZB))):
                            i4 = 2 * (ci % 3) + m
                            cx.pe.deps([vb_, pTb[i4], one1b], [Ux[1], Zx[1]])
                            nc.tensor.matmul(Ux[0][:], lhsT=vt[:, c, :], rhs=pT[i4][:], start=first, stop=last)
                            i3 = nc.tensor.matmul(Zx[0][:], lhsT=one1[:], rhs=pT[i4][:], start=first,
                                                  stop=last)
                            cx.pe.finish(i3, [vb_, pTb[i4], one1b], [Ux[1], Zx[1]] if last else [])

                    LA = 2
                    it = chunk_iter()
                    pend = []
                    for ci0 in range(min(LA, nch)):
                        cd = next(it)
                        pend.append(cd)
                        stage_S(ci0, cd[0], cd[1], cd[4])
                    for ci in range(nch):
                        stage_D(ci)
                        if ci + LA < nch:
                            cd = next(it)
                            pend.append(cd)
                            stage_S(ci + LA, cd[0], cd[1], cd[4])
                        cur = pend.pop(0)
                        stage_P(ci, cur[2], cur[3], cur[4])
                    rA, rB, oA, oB = ep
                    cx.dve.op(lambda e: e.reciprocal(out=rA[:], in_=ZA[0][:]), [ZA[1]], [epb[0]])
                    cx.dve.op(lambda e: e.reciprocal(out=rB[:], in_=ZB[0][:]), [ZB[1]], [epb[1]])
                    cx.dve.op(lambda e: e.tensor_tensor(out=oA[:], in0=UA[0][:], in1=rA[:], op=ALU.mult),
                              [UA[1], epb[0]], [epb[2]])
                    cx.dve.op(lambda e: e.tensor_tensor(out=oB[:], in0=UB[0][:], in1=rB[:], op=ALU.mult),
                              [UB[1], epb[1]], [epb[3]])
                    cx.dve.op(lambda e: e.scalar_tensor_tensor(out=oA[:], in0=oB[:], scalar=nlam[:, 0:1],
                                                               in1=oA[:], op0=ALU.mult, op1=ALU.add),
                              [epb[3], epb[2], nlamb], [epb[2]])
                    cx.act.op(lambda e: e.activation(out=osq[:], in_=oA[:], func=AF.Square), [epb[2]], [osqb])
                    pt, pb = PS[0]
                    cx.mm_group(pb, [(pt[:], onee[:], osq[:])], [osqb, oneeb])
                    cx.act.op(lambda e: e.activation(out=rA[:], in_=pt[:], func=AF.Ln, bias=eps_t[:, 1:2],
                                                     scale=1.0), [pb, epsb], [epb[0]])
                    cx.act.op(lambda e: e.activation(out=rA[:], in_=rA[:], func=AF.Exp, scale=-0.5),
                              [epb[0]], [epb[0]])
                    cx.dve.op(lambda e: e.scalar_tensor_tensor(out=OT[:, hd, :], in0=oA[:], scalar=pbt[:, 16:17],
                                                               in1=rA[:], op0=ALU.mult, op1=ALU.mult),
                              [epb[2], epb[0], pbb], [OTb[hd]])
                k = 0
                for u in range(2):
                    wt, wb = ws.get()[0]
                    for jj in range(4):
                        n = u * 4 + jj
                        pt, pb = PS[k % 4]
                        k += 1
                        cx.mm_group(pb, [(pt[:], wt[:, c * 512 + jj * 128: c * 512 + jj * 128 + 128], OT[:, c, :])
                                         for c in range(NH)], [wb] + OTb)
                        cx.dve.op(lambda e: e.tensor_tensor(out=xt[:, n, 0:T], in0=xt[:, n, 0:T], in1=pt[:],
                                                            op=ALU.add), [pb, xwb], [xwb])
                rms_stats(cx, xt, xwb, 0, T, sq, sqb, [PS[0]], eps_t, rstd, rstdb, ones, onesb)
                apply_norm(cx, xt, xb, 0, T, lambda kc: pbt[:, kc:kc + 1], rstd, rstdb, h, hb)
                mlp_block(cx, xt, xb, 0, h, hb, ws, PS[1:4], PS[4:8], relu_t, relu_b, aT, aTb)
                if not final:
                    cx.pool.dma(XL[t], xt[:, :, 0:T], [xwb], [XLb[t]], osem[0])
                    cx.pool.dma(ELv[t, 0], xt[:, :, 0:HALO], [xwb], [ELb], osem[4])
                    cx.pool.dma(ELv[t, 1], xt[:, :, T - HALO:T], [xwb], [ELb], osem[5])
                else:
                    rms_stats(cx, xt, xwb, 0, T, sq, sqb, [PS[0]], eps_t, rstd, rstdb, ones, onesb)
                    for kc in range(KC):
                        yk, ykb = ep[kc % 4], epb[kc % 4]
                        cx.dve.op(lambda e: e.scalar_tensor_tensor(out=yk[:], in0=xt[:, kc, 0:T],
                                                                   scalar=pbt[:, 8 + kc:9 + kc], in1=rstd[:, 0:T],
                                                                   op0=ALU.mult, op1=ALU.mult),
                                  [xwb, rstdb], [ykb])
                        cx.pool.dma(yout[t, :, kc, :], yk[:], [ykb], [], so_y[kc % 4])
            cx.barrier()
        if not final:
            gather(EL, ELb, GE1, GE1b, GE2, GE2b)

    phase_A(0, True)
    phase_B(1, False)
    phase_A(2, False)
    phase_B(3, True)
    cx.finish_outputs()
    return nc


def units_k1024(w):
    n = w.shape[1]
    return np.ascontiguousarray(w.reshape(KC, 128, n // 512, 512).transpose(2, 1, 0, 3)).reshape(n // 512, 128, UNIT)


def units_w2(w):
    return np.ascontiguousarray(w.reshape(4, 8, 128, 2, 512).transpose(0, 3, 2, 1, 4)).reshape(8, 128, UNIT)


_CACHE = {}


def run_model(inp, seqs):
    ncores = 8
    ns = len(seqs)
    slen = [x.shape[0] for x in seqs]
    tiles_per = [s // ncores // T for s in slen]
    tile_seq = [s for s in range(ns) for _ in range(tiles_per[s])]
    nt = len(tile_seq)
    cores = list(range(ncores))

    ulist = []
    for i in range(4):
        j = i // 2
        if i % 2 == 0:
            ulist += list(units_k1024(inp["mix_in_w"][j])) + list(units_k1024(inp["mix_out_w"][j]))
        else:
            ulist += list(units_k1024(inp["attn_qkv_w"][j])) + list(units_k1024(inp["attn_out_w"][j]))
        ulist += list(units_k1024(inp["mlp_w1"][i])) + list(units_w2(inp["mlp_w2"][i]))
    wsrc = np.stack(ulist).astype(np.float32)
    assert wsrc.shape[0] == NU

    def col(v):
        return v.reshape(KC, 128).T

    def tile_start(c, t):
        s = tile_seq[t]
        k = t - tile_seq.index(s)
        return s, c * (slen[s] // ncores) + k * T

    xs = [np.asarray(x, np.float32) for x in seqs]
    xh = np.zeros((ncores, nt, 128, KC, TP), np.float32)
    cnts = np.zeros((ncores, nt, 128, 4, T), np.float32)
    dls = np.zeros((ncores, 128, sum(slen[tile_seq[t]] // 128 for t in range(nt))), np.float32)
    hms = np.zeros((ncores, 128, 16), np.float32)
    for c in cores:
        o = 0
        if c > 0:
            hms[c, :, c - 1] = 1.0
        if c < ncores - 1:
            hms[c, :, 8 + c + 1] = 1.0
        for t in range(nt):
            s, t0 = tile_start(c, t)
            S = slen[s]
            lo, hi = t0 - HALO, t0 + T + HALO
            a, b = max(lo, 0), min(hi, S)
            blk = np.zeros((TP, D), np.float32)
            blk[a - lo:b - lo] = xs[s][a:b]
            xh[c, t] = blk.T.reshape(KC, 128, TP).transpose(1, 0, 2)
            tt = np.arange(t0, t0 + T)
            for g, w in enumerate(WINDOWS):
                cnts[c, t, :, g, :] = (np.clip(tt + w // 2, 0, S) - np.clip(tt - w // 2, 0, S))[None, :]
            n = S // 128
            dls[c, :, o:o + n] = (t0 - 128 * np.arange(n))[None, :]
            o += n
    pa = np.zeros((2, 128, NPA), np.float32)
    pw = np.zeros((2, 128, 4, 128), np.float32)
    pb = np.zeros((2, 128, NPB), np.float32)
    for j in range(2):
        i = 2 * j
        pa[j, :, 0:8] = col(inp["norm1_g"][i])
        pa[j, :, 8:16] = col(inp["norm2_g"][i])
        pa[j, :, 16:24] = col(inp["norm1_g"][i + 1])
        pa[j, :, 24:28] = inp["pool_scale"][j].reshape(4, 128).T
        for tap in range(3):
            pa[j, :, 28 + 4 * tap:32 + 4 * tap] = inp["conv_w"][j][tap].reshape(4, 128).T
        pw[j] = inp["pool_w"][j].transpose(1, 0, 2)
        li = 0.8 - 0.6 * math.exp(-0.3 * (i + 1))
        pb[j, :, 0:8] = col(inp["norm2_g"][i + 1])
        pb[j, :, 8:16] = col(inp["final_g"])
        pb[j, :, 16] = inp["subln_g"][j] * np.float32(1.0 - li)
        pb[j, :, 17:17 + 64] = inp["lambda_q1"][j][None]
        pb[j, :, 17 + 64:17 + 128] = inp["lambda_k1"][j][None]
        pb[j, :, 17 + 128:17 + 192] = inp["lambda_q2"][j][None]
        pb[j, :, 17 + 192:17 + 256] = inp["lambda_k2"][j][None]
        pb[j, :, 17 + 256] = li
    d0np = (np.arange(1408, dtype=np.float32)[None, :] - np.arange(128, dtype=np.float32)[:, None])
    cngnp = np.tile((-128.0 * (8 * np.arange(16, dtype=np.float32) + 7))[None, :], (128, 1))
    key = ("F", tuple(tile_seq), tuple(slen))
    if key not in _CACHE:
        _CACHE[key] = build_fused(tile_seq, slen)
    ncF = _CACHE[key]
    in_maps = [{"xin": xh[c], "cnt": cnts[c], "wsrc": wsrc, "pa": pa, "pw": pw, "pb": pb, "dl": dls[c],
                "d0": d0np, "cng": cngnp, "hm": hms[c]} for c in cores]
    res = run_bass_kernel_spmd(ncF, in_maps, core_ids=cores).results
    out = [np.zeros((slen[s], D), np.float32) for s in range(ns)]
    for c in cores:
        yo = res[c]["yout"]
        for t in range(nt):
            s, t0 = tile_start(c, t)
            out[s][t0:t0 + T] = yo[t].transpose(1, 0, 2).reshape(D, T).T
    return out


def kernel(**inputs):
    inp = {k: np.asarray(v) for k, v in inputs.items()}
    xp = inp["x_prompt"]
    xsm = inp["x_sample"]
    seqs = [xp[b] for b in range(xp.shape[0])] + [xsm[b] for b in range(xsm.shape[0])]
    ys = run_model(inp, seqs)
    nb = xp.shape[0]
    return (np.stack(ys[:nb]).astype(np.float32), np.stack(ys[nb:]).astype(np.float32))
```
